# Optimizing a Trainium2 kernel written in Bass

```python
import math
import jax, jax.numpy as jnp
from jax import lax
import numpy as np

D_MODEL = 4096
BATCH = 2
SEQ = 8192
DEPTH = 1

CHUNK = 64
HEAD_DIM = 128
N_HEADS_A = 16
N_HEADS_B = 16
D_A = N_HEADS_A * HEAD_DIM
D_B = N_HEADS_B * HEAD_DIM
N_IDX_HEADS = 32
IDX_DIM = 64
TOPK_MAX = 256
N_BUCKETS = 32
MAX_DISTANCE = 1024
D_FF = 4 * D_MODEL
Q_BLOCK = 128
EPS = 1e-6
SPLIT_SIZES = (D_A, D_A, D_A, N_IDX_HEADS * IDX_DIM, IDX_DIM, N_IDX_HEADS,
               D_B, D_B, D_B, N_HEADS_B, D_MODEL, D_MODEL)
D_IN = sum(SPLIT_SIZES)

kernel_name = 'hybrid_dsa_fox_gated_block'


def rmsnorm(x, g):
    x32 = x.astype(jnp.float32)
    y = x32 * lax.rsqrt(jnp.mean(x32 * x32, axis=-1, keepdims=True) + EPS)
    return y.astype(x.dtype) * g


def t5_bucket(rel):
    half = N_BUCKETS // 2
    max_exact = half // 2
    base = jnp.where(rel > 0, half, 0)
    n = jnp.abs(rel)
    nf = jnp.maximum(n, max_exact).astype(jnp.float32)
    large = max_exact + (jnp.log(nf / max_exact) / math.log(MAX_DISTANCE / max_exact)
                         * (half - max_exact)).astype(jnp.int32)
    large = jnp.minimum(large, half - 1)
    return base + jnp.where(n < max_exact, n, large)


def dsa_attention(q, k, v, iq, ik, iw, rel_bias):
    B, S = q.shape[0], q.shape[1]
    n_blocks = S // CHUNK
    top_k = min(TOPK_MAX, S // 4)
    key_pos = jnp.arange(S, dtype=jnp.int32)
    b_idx = jnp.arange(B)[:, None, None]
    scale = HEAD_DIM ** -0.5

    def block(c):
        start = c * CHUNK
        qb = lax.dynamic_slice_in_dim(q, start, CHUNK, axis=1)
        iqb = lax.dynamic_slice_in_dim(iq, start, CHUNK, axis=1)
        iwb = lax.dynamic_slice_in_dim(iw, start, CHUNK, axis=1)
        sc = jnp.einsum('bthd,bsd->bths', iqb, ik) * (IDX_DIM ** -0.5)
        score = jnp.einsum('bths,bth->bts', jax.nn.relu(sc), iwb).astype(jnp.float32)
        limit = start + CHUNK
        score = jnp.where(key_pos[None, None, :] < limit, score, -jnp.inf)
        _, idx = lax.top_k(score, top_k)
        valid = idx < limit
        kg = k[b_idx, idx]
        vg = v[b_idx, idx]
        logits = jnp.einsum('bthd,btkhd->bhtk', qb, kg).astype(jnp.float32) * scale
        q_pos = start + jnp.arange(CHUNK, dtype=jnp.int32)
        bias = rel_bias[t5_bucket(idx - q_pos[None, :, None])]
        logits = logits + jnp.transpose(bias, (0, 3, 1, 2)).astype(jnp.float32)
        logits = jnp.where(valid[:, None, :, :], logits, -jnp.inf)
        p = jax.nn.softmax(logits, axis=-1)
        return jnp.einsum('bhtk,btkhd->bthd', p.astype(v.dtype), vg)

    out = lax.map(block, jnp.arange(n_blocks))
    out = jnp.transpose(out, (1, 0, 2, 3, 4))
    return out.reshape(B, S, N_HEADS_A * HEAD_DIM)


def fox_attention(q, k, v, log_f):
    B, S = q.shape[0], q.shape[1]
    n_blocks = S // Q_BLOCK
    cum = jnp.transpose(lax.cumsum(log_f, axis=1), (0, 2, 1))
    key_pos = jnp.arange(S, dtype=jnp.int32)
    scale = HEAD_DIM ** -0.5

    def block(i):
        start = i * Q_BLOCK
        qb = lax.dynamic_slice_in_dim(q, start, Q_BLOCK, axis=1)
        cq = lax.dynamic_slice_in_dim(cum, start, Q_BLOCK, axis=2)
        logits = jnp.einsum('bthd,bshd->bhts', qb, k).astype(jnp.float32) * scale
        logits = logits + cq[:, :, :, None] - cum[:, :, None, :]
        q_pos = start + jnp.arange(Q_BLOCK, dtype=jnp.int32)
        logits = jnp.where(key_pos[None, :] <= q_pos[:, None], logits, -jnp.inf)
        p = jax.nn.softmax(logits, axis=-1)
        return jnp.einsum('bhts,bshd->bthd', p.astype(v.dtype), v)

    out = lax.map(block, jnp.arange(n_blocks))
    out = jnp.transpose(out, (1, 0, 2, 3, 4))
    return out.reshape(B, S, N_HEADS_B * HEAD_DIM)


def setup_inputs(seed: int = 0) -> dict:
    key = jax.random.key(seed)
    ks = jax.random.split(key, 12)
    f32 = jnp.float32
    x = jax.random.normal(ks[0], (BATCH, SEQ, D_MODEL), f32)
    w_in = jax.random.normal(ks[1], (DEPTH, D_MODEL, D_IN), f32) * D_MODEL ** -0.5
    b_f = 2.0 + 0.5 * jax.random.normal(ks[2], (DEPTH, N_HEADS_B), f32)
    w_out_a = jax.random.normal(ks[3], (DEPTH, D_A, D_MODEL), f32) * D_A ** -0.5
    w_out_b = jax.random.normal(ks[4], (DEPTH, D_B, D_MODEL), f32) * D_B ** -0.5
    w_o = jax.random.normal(ks[5], (DEPTH, D_MODEL, D_MODEL), f32) * D_MODEL ** -0.5
    w_up = jax.random.normal(ks[6], (DEPTH, D_MODEL, D_FF), f32) * D_MODEL ** -0.5
    w_down = jax.random.normal(ks[7], (DEPTH, D_FF, D_MODEL), f32) * D_FF ** -0.5
    g_mix = 1.0 + 0.01 * jax.random.normal(ks[8], (DEPTH, D_MODEL), f32)
    g_mlp = 1.0 + 0.01 * jax.random.normal(ks[9], (DEPTH, D_MODEL), f32)
    g_final = 1.0 + 0.01 * jax.random.normal(ks[10], (D_MODEL,), f32)
    rel_bias = 0.5 * jax.random.normal(ks[11], (N_BUCKETS, N_HEADS_A), f32)
    return {'x': x, 'w_in': w_in, 'b_f': b_f, 'w_out_a': w_out_a, 'w_out_b': w_out_b,
            'w_o': w_o, 'w_up': w_up, 'w_down': w_down, 'g_mix': g_mix, 'g_mlp': g_mlp,
            'g_final': g_final, 'rel_bias': rel_bias}


def reference(x, w_in, b_f, w_out_a, w_out_b, w_o, w_up, w_down, g_mix, g_mlp, g_final, rel_bias):
    B, S = x.shape[0], x.shape[1]
    split_points = [int(p) for p in np.cumsum(SPLIT_SIZES)[:-1]]
    for l in range(DEPTH):
        xn = rmsnorm(x, g_mix[l])
        proj = jnp.einsum('bsd,de->bse', xn, w_in[l])
        qa, ka, va, iq, ik, iw, qb, kb, vb, fl, ga, gb = jnp.split(proj, split_points, axis=-1)
        ya = dsa_attention(qa.reshape(B, S, N_HEADS_A, HEAD_DIM),
                           ka.reshape(B, S, N_HEADS_A, HEAD_DIM),
                           va.reshape(B, S, N_HEADS_A, HEAD_DIM),
                           iq.reshape(B, S, N_IDX_HEADS, IDX_DIM), ik,
                           iw * (N_IDX_HEADS ** -0.5), rel_bias)
        log_f = jax.nn.log_sigmoid(fl.astype(jnp.float32) + b_f[l].astype(jnp.float32))
        yb = fox_attention(qb.reshape(B, S, N_HEADS_B, HEAD_DIM),
                           kb.reshape(B, S, N_HEADS_B, HEAD_DIM),
                           vb.reshape(B, S, N_HEADS_B, HEAD_DIM), log_f)
        pa = jnp.einsum('bse,ed->bsd', ya, w_out_a[l])
        pb = jnp.einsum('bse,ed->bsd', yb, w_out_b[l])
        merged = jax.nn.sigmoid(ga) * pa + jax.nn.sigmoid(gb) * pb
        x = x + jnp.einsum('bsd,de->bse', merged, w_o[l])
        hn = rmsnorm(x, g_mlp[l])
        h = jnp.square(jax.nn.relu(jnp.einsum('bsd,df->bsf', hn, w_up[l])))
        x = x + jnp.einsum('bsf,fd->bsd', h, w_down[l])
    return rmsnorm(x, g_final)
```

```python
import math
from contextlib import ExitStack
import numpy as np
import ml_dtypes
import concourse.bass as bass
import concourse.mybir as mybir
from concourse.bass_utils import run_bass_kernel_spmd

F32 = mybir.dt.float32
BF16 = mybir.dt.bfloat16
U8 = mybir.dt.uint8
ALU = mybir.AluOpType
AF = mybir.ActivationFunctionType

REPL = -float(2.0 ** 100)
NEGFILL = -float(2.0 ** 101)
EPS = 1e-6
SAME_ENG_WAIT = True
NO_SELF_WAIT = ("pe", "act")


class Cfg:
    def __init__(self, S=8192, D=4096, H=16, HI=32, DFF=16384, TOPK=256):
        self.S, self.D, self.H, self.HI, self.DFF, self.TOPK = S, D, H, HI, DFF, TOPK
        self.DH, self.DI, self.NBK, self.MAXD = 128, 64, 32, 1024
        self.DA = H * 128
        self.NTG = S // 128
        self.NT = self.NTG // 4
        self.NG = self.NT // 4
        self.TL = self.NT * 128
        self.DC = D // 128
        self.FC = DFF // 128
        self.DIN = 3 * self.DA + HI * 64 + 64 + HI + 3 * self.DA + H + 2 * D


class Buf:
    __slots__ = ("name", "w", "r")

    def __init__(self, name=""):
        self.name, self.w, self.r = name, None, {}


class Op:
    __slots__ = ("eng", "fn", "deps", "dma_key", "needs_inc", "inc_val", "gid", "is_dma", "inc_amt")


def _dom(o):
    return ("dma", o.dma_key) if o.is_dma else ("eng", o.eng)


class Prog:
    ENGS = ("pe", "act", "dve", "pool", "sp")

    def __init__(self):
        self.streams = {e: [] for e in self.ENGS}
        self.dma_cnt = {}
        self.dma_last = {}
        self.gid = 0
        self.pending = {e: [] for e in self.ENGS}

    def op(self, eng, fn, reads=(), writes=(), dma_key=None, inc_amt=None):
        o = Op()
        o.eng, o.fn, o.dma_key = eng, fn, dma_key
        o.is_dma = dma_key is not None
        o.needs_inc = o.is_dma
        o.gid = self.gid
        self.gid += 1
        deps = {}

        def add(d):
            if d is None:
                return
            k = _dom(d)
            c = deps.get(k)
            if c is None or d.gid > c.gid:
                deps[k] = d

        for b in reads:
            add(b.w)
        for b in writes:
            add(b.w)
            for d in b.r.values():
                add(d)
        for d in self.pending[eng]:
            add(d)
        self.pending[eng] = []
        for k_, d in deps.items():
            if k_ == ("eng", eng) and (eng in NO_SELF_WAIT or not SAME_ENG_WAIT):
                continue
            d.needs_inc = True
        o.deps = deps
        if o.is_dma:
            o.inc_amt = 16 if inc_amt is None else inc_amt
            self.dma_cnt[dma_key] = self.dma_cnt.get(dma_key, 0) + o.inc_amt
            o.inc_val = self.dma_cnt[dma_key]
            self.dma_last[dma_key] = o
        k = _dom(o)
        for b in reads:
            b.r[k] = o
        for b in writes:
            b.w = o
            b.r = {}
        self.streams[eng].append(o)
        return o

    def barrier(self, skip_keys=()):
        lasts = []
        for st_ in self.streams.values():
            for o in reversed(st_):
                if not (o.is_dma and o.dma_key in skip_keys):
                    lasts.append(o)
                    break
        lasts += [o for k, o in self.dma_last.items() if k not in skip_keys]
        for e in self.ENGS:
            self.pending[e] = list(lasts)

    def emit(self, nc, st):
        for e in self.ENGS:
            c = 0
            for o in self.streams[e]:
                if not o.is_dma and o.needs_inc:
                    c += 1
                    o.inc_val = c
        sems = {}
        for e in self.ENGS:
            sems[("eng", e)] = st.enter_context(nc.semaphore("s_" + e))
        for i, k in enumerate(self.dma_cnt):
            sems[("dma", k)] = st.enter_context(nc.semaphore("d%d" % i))
        block = st.enter_context(nc.Block())
        prog = self

        def mk(e, final):
            def body(eng):
                known = {}
                for o in prog.streams[e]:
                    for k, d in o.deps.items():
                        if k == ("eng", e) and (e in NO_SELF_WAIT or not SAME_ENG_WAIT):
                            continue
                        v = d.inc_val
                        if known.get(k, 0) >= v:
                            continue
                        eng.wait_ge(sems[k], v)
                        known[k] = v
                    ins = o.fn(eng)
                    if o.needs_inc:
                        ins.then_inc(sems[_dom(o)], o.inc_amt if o.is_dma else 1)
                if final:
                    for k, v in prog.dma_cnt.items():
                        if known.get(("dma", k), 0) < v:
                            eng.wait_ge(sems[("dma", k)], v)
            return body

        block.sync(mk("sp", True))
        block.scalar(mk("act", False))
        block.vector(mk("dve", False))
        block.gpsimd(mk("pool", False))
        block.tensor(mk("pe", False))


class Mem:
    def __init__(self, big, size):
        self.big, self.size, self.off = big, size, 0

    def alloc(self, nbytes, dtype, shape=None):
        rb = (nbytes + 63) // 64 * 64
        assert self.off + rb <= self.size, ("SBUF overflow", self.off, rb, self.size)
        v = self.big[:, self.off:self.off + nbytes].bitcast(dtype)
        self.off += rb
        return v

    def alloc_top(self, nbytes, dtype):
        rb = (nbytes + 63) // 64 * 64
        self.size -= rb
        assert self.off <= self.size
        return self.big[:, self.size:self.size + nbytes].bitcast(dtype)

    def mark(self):
        return self.off

    def release(self, m):
        self.off = m


class Tl:
    __slots__ = ("v", "b")

    def __init__(self, v, name=""):
        self.v, self.b = v, Buf(name)


def bf(mem, n, name=""):
    return Tl(mem.alloc(n * 2, BF16), name)


def f32(mem, n, name=""):
    return Tl(mem.alloc(n * 4, F32), name)


def build(cfg, debug=False):
    S, D, H, HI, DFF = cfg.S, cfg.D, cfg.H, cfg.HI, cfg.DFF
    DA, NT, NG, TL, DC, FC, NTG = cfg.DA, cfg.NT, cfg.NG, cfg.TL, cfg.DC, cfg.FC, cfg.NTG
    NTA = 4 * NT
    nc = bass.Bass("TRN2", target_bir_lowering=False)
    P = Prog()

    def din(name, shape, dt=F32):
        return nc.dram_tensor(name, list(shape), dt, kind="ExternalInput").ap()

    def dscr(name, shape, dt=BF16):
        kind = "ExternalOutput" if debug else "Internal"
        return nc.dram_tensor(name, list(shape), dt, kind=kind)

    x_in = din("x", [TL, D])
    w_in = din("w_in", [D, cfg.DIN])
    w_oa = din("w_out_a", [DA, D])
    w_ob = din("w_out_b", [DA, D])
    w_o = din("w_o", [D, D])
    w_up = din("w_up", [D, DFF])
    w_dn = din("w_down", [DFF, D])
    bfv = din("b_f", [1, H])
    gmixT = din("g_mixT", [128, DC])
    gmlpT = din("g_mlpT", [128, DC])
    gfin = din("g_final", [D])
    rb15 = din("rb15", [1, H])
    EB = din("EB", [H, 128, 3, 4, 128])
    sel_in = din("sel", [1, 4])
    negv_in = din("negv", [128, 4, 128])
    valid_in = din("valid", [128, 4, 128])
    cm_in = din("cm", [128, 4, 128])
    out = nc.dram_tensor("out", [TL, D], F32, kind="ExternalOutput").ap()

    o_qa, o_ka, o_va = 0, DA, 2 * DA
    o_iq = 3 * DA
    o_ik = o_iq + HI * 64
    o_iw = o_ik + 64
    o_qb = o_iw + HI
    o_kb, o_vb = o_qb + DA, o_qb + 2 * DA
    o_fl = o_qb + 3 * DA
    o_ga = o_fl + H
    o_gb = o_ga + D
    blocks = []

    def addblk(kind, c0, total):
        for i in range(0, total, 512):
            blocks.append((kind, i // 512, c0 + i, min(512, total - i)))

    addblk("qa", o_qa, DA); addblk("ka", o_ka, DA); addblk("va", o_va, DA)
    addblk("iq", o_iq, HI * 64); blocks.append(("misc", 0, o_ik, 64 + HI))
    addblk("qb", o_qb, DA); addblk("kb", o_kb, DA); addblk("vb", o_vb, DA)
    blocks.append(("fl", 0, o_fl, H))
    addblk("ga", o_ga, D); addblk("gb", o_gb, D)
    NBI = len(blocks)

    Wt_in = dscr("Wt_in", [NBI, 128, DC, 512])
    Wt_oa = dscr("Wt_oa", [D // 512, 128, H, 512])
    Wt_ob = dscr("Wt_ob", [D // 512, 128, H, 512])
    Wt_o = dscr("Wt_o", [D // 512, 128, DC, 512])
    Wt_up = dscr("Wt_up", [DFF // 512, 128, DC, 512])
    Wt_dn = dscr("Wt_dn", [D // 512, 128, FC, 512])
    QaT = dscr("QaT", [DA, TL]); QbT = dscr("QbT", [DA, TL])
    IqT = dscr("IqT", [HI * 64, TL])
    SgA = dscr("SgA", [D, TL]); SgB = dscr("SgB", [D, TL])
    KaT_i = nc.dram_tensor("KaT_i", [DA, TL], BF16); KbT_i = nc.dram_tensor("KbT_i", [DA, TL], BF16)
    Va_i = nc.dram_tensor("Va_i", [TL, DA], BF16); Vb_i = nc.dram_tensor("Vb_i", [TL, DA], BF16)
    Ik_i = nc.dram_tensor("Ik_i", [64, TL], BF16); Fl_i = nc.dram_tensor("Fl_i", [TL, H], F32)
    KaT_g = nc.dram_tensor("KaT_g", [4 * DA, TL], BF16); KbT_g = nc.dram_tensor("KbT_g", [4 * DA, TL], BF16)
    Va_g = nc.dram_tensor("Va_g", [4 * TL, DA], BF16); Vb_g = nc.dram_tensor("Vb_g", [4 * TL, DA], BF16)
    Ik_g = nc.dram_tensor("Ik_g", [4 * 64, TL], BF16); Fl_g = nc.dram_tensor("Fl_g", [4 * TL, H], F32)
    dbg = {}
    if debug:
        dbg["ya"] = nc.dram_tensor("dbg_ya", [NG, 128, H, 512], BF16, kind="ExternalOutput")
        dbg["yb"] = nc.dram_tensor("dbg_yb", [NG, 128, H, 512], BF16, kind="ExternalOutput")
        dbg["mask"] = nc.dram_tensor("dbg_mask", [NG, 128, NTG, 512], BF16, kind="ExternalOutput")
        dbg["cpos"] = nc.dram_tensor("dbg_cpos", [128, NTA, H], F32, kind="ExternalOutput")
        dbg["kag"] = nc.dram_tensor("dbg_kag", [4 * DA, TL], BF16, kind="ExternalOutput")
    dram_b = {}

    def DB(name):
        if name not in dram_b:
            dram_b[name] = Buf(name)
        return dram_b[name]

    st = ExitStack()
    SB_BYTES = 207 * 1024
    big = st.enter_context(nc.sbuf_tensor("big", [128, SB_BYTES], U8))
    mem = Mem(big, SB_BYTES)
    banks = []
    for i in range(8):
        t = st.enter_context(nc.psum_tensor("ps%d" % i, [128, 512], F32))
        banks.append(Tl(t[:, :], "ps%d" % i))
    bank_rr = [0]

    def nbank(lo=0, hi=8):
        i = lo + bank_rr[0] % (hi - lo)
        bank_rr[0] += 1
        return banks[i]

    rr = {"ev": 0, "cast": 0}

    def dma(outap, inap, reads, writes, key, q="sp"):
        P.op(q, lambda e: e.dma_start(out=outap, in_=inap), reads, writes, dma_key=key)

    def mm(outap, lhsT, rhs, start, stop, reads, writes):
        P.op("pe", lambda e: e.matmul(outap, lhsT, rhs, start=start, stop=stop), reads, writes)

    def tr(outap, inap, ident, reads, writes):
        P.op("pe", lambda e: e.transpose(outap, inap, ident), reads, writes)

    def act(outap, inap, func, reads, writes, bias=None, scale=None, accum=None):
        kw = {}
        if bias is not None:
            kw["bias"] = bias
        if scale is not None:
            kw["scale"] = scale
        if accum is not None:
            kw["accum_out"] = accum
        P.op("act", lambda e: e.activation(outap, inap, func, **kw), reads, writes)

    def ts(eng, outap, inap, s1, s2, op0, op1, reads, writes):
        if op1 is None:
            P.op(eng, lambda e: e.tensor_scalar(outap, inap, s1, None, op0), reads, writes)
        else:
            P.op(eng, lambda e: e.tensor_scalar(outap, inap, s1, s2, op0, op1), reads, writes)

    def tt(eng, outap, a, b, op, reads, writes):
        P.op(eng, lambda e: e.tensor_tensor(outap, a, b, op), reads, writes)

    def stt(eng, outap, in0, scalar, in1, op0, op1, reads, writes):
        P.op(eng, lambda e: e.scalar_tensor_tensor(outap, in0, scalar, in1, op0, op1), reads, writes)

    def cp(eng, outap, inap, reads, writes):
        if eng == "act":
            P.op("act", lambda e: e.activation(outap, inap, AF.Copy), reads, writes)
        else:
            P.op(eng, lambda e: e.tensor_copy(outap, inap), reads, writes)

    def evac_eng():
        rr["ev"] += 1
        return ("act", "dve")[rr["ev"] % 2]

    ident = bf(mem, 128, "ident")
    ones_bf = bf(mem, 128, "ones_bf")
    ones_f = f32(mem, 128, "ones_f")
    tri_f = f32(mem, 128, "tri_f")
    gmix_sb = f32(mem, DC, "gmix"); gmlp_sb = f32(mem, DC, "gmlp")
    ss = f32(mem, NT, "ss"); rstd = f32(mem, NT, "rstd")
    wfin = f32(mem, NT * HI, "wfin"); wabs = f32(mem, NT * HI, "wabs"); wsgn = f32(mem, NT * HI, "wsgn")
    sel_sb = f32(mem, 4, "sel")
    erb15 = f32(mem, H, "erb15")
    bf_bc = f32(mem, H, "bf_bc")
    negv = f32(mem, 512, "negv")
    valid = bf(mem, 512, "valid")
    cm = bf(mem, 512, "cm")
    tmp512 = f32(mem, 512, "tmp512")
    Cpos = f32(mem, NTA * H, "Cpos")
    Cref = f32(mem, NT * H, "Cref")
    Cpos3 = Cpos.v.rearrange("p (n h) -> p n h", h=H)
    Cref3 = Cref.v.rearrange("p (n h) -> p n h", h=H)
    wabs3 = wabs.v.rearrange("p (n h) -> p n h", h=HI)
    wsgn3 = wsgn.v.rearrange("p (n h) -> p n h", h=HI)
    wfin3 = wfin.v.rearrange("p (n h) -> p n h", h=HI)

    P.op("pool", lambda e: e.memset(ones_bf.v, 1.0), (), (ones_bf.b,))
    P.op("pool", lambda e: e.memset(ones_f.v, 1.0), (), (ones_f.b,))
    P.op("pool", lambda e: e.memset(tri_f.v, 1.0), (), (tri_f.b,))
    P.op("pool", lambda e: e.affine_select(tri_f.v, tri_f.v, [[1, 128]], ALU.is_ge, 0.0, base=0,
                                           channel_multiplier=-1), (tri_f.b,), (tri_f.b,))
    P.op("pool", lambda e: e.memset(tmp512.v[:, 0:128], 1.0), (), (tmp512.b,))
    P.op("pool", lambda e: e.affine_select(tmp512.v[:, 0:128], tmp512.v[:, 0:128], [[1, 128]], ALU.is_equal, 0.0,
                                           base=0, channel_multiplier=-1), (tmp512.b,), (tmp512.b,))
    cp("dve", ident.v, tmp512.v[:, 0:128], (tmp512.b,), (ident.b,))
    P.op("dve", lambda e: e.memset(ss.v, 0.0), (), (ss.b,))
    dma(gmix_sb.v, gmixT, (), (gmix_sb.b,), "c0")
    dma(gmlp_sb.v, gmlpT, (), (gmlp_sb.b,), "c1")
    dma(sel_sb.v, sel_in.partition_broadcast(128), (), (sel_sb.b,), "c2")
    dma(erb15.v, rb15.partition_broadcast(128), (), (erb15.b,), "c3")
    dma(bf_bc.v, bfv.partition_broadcast(128), (), (bf_bc.b,), "c4")
    dma(negv.v.rearrange("p (r s) -> p r s", r=4), negv_in, (), (negv.b,), "c5")
    dma(tmp512.v.rearrange("p (r s) -> p r s", r=4), valid_in, (tmp512.b,), (tmp512.b,), "c6")
    cp("dve", valid.v, tmp512.v, (tmp512.b,), (valid.b,))
    dma(tmp512.v.rearrange("p (r s) -> p r s", r=4), cm_in, (tmp512.b,), (tmp512.b,), "c6")
    cp("dve", cm.v, tmp512.v, (tmp512.b,), (cm.b,))
    act(erb15.v, erb15.v, AF.Exp, (erb15.b,), (erb15.b,))
    base_mark = mem.mark()

    class Precast:
        def __init__(self, KH, NSL, pfx):
            self.KH, self.NSL, self.pfx = KH, NSL, pfx
            self.sf = [f32(mem, KH * 512, pfx + "f%d" % i) for i in range(NSL)]
            self.sb = [bf(mem, KH * 512, pfx + "b%d" % i) for i in range(NSL)]
            self.items = []
            self.n = 0
            self.look = NSL - 1
            self.store_q = "sp"

        def add(self, Wsrc, Wdst, nrows, c0, ncols, blk, gsb):
            KC = nrows // 128
            for k0 in range(0, KC, self.KH):
                self.items.append((Wsrc, Wdst, c0, ncols, blk, gsb, k0, min(self.KH, KC - k0)))

        def load(self, n):
            Wsrc, Wdst, c0, ncols, blk, gsb, k0, kn = self.items[n]
            i = n % self.NSL
            sfv = self.sf[i].v.rearrange("p (k c) -> p k c", c=512)
            src = Wsrc.rearrange("(kc p) c -> p kc c", p=128)
            dma(sfv[:, 0:kn, 0:ncols], src[:, k0:k0 + kn, c0:c0 + ncols], (), (self.sf[i].b,),
                "%sl%d" % (self.pfx, i))

        def cast_store(self, n):
            Wsrc, Wdst, c0, ncols, blk, gsb, k0, kn = self.items[n]
            i = n % self.NSL
            sf, sb_ = self.sf[i], self.sb[i]
            sfv = sf.v.rearrange("p (k c) -> p k c", c=512)
            sbv = sb_.v.rearrange("p (k c) -> p k c", c=512)
            if gsb is None:
                for (ka, kb_) in ((0, kn // 2), (kn // 2, kn)):
                    if kb_ > ka:
                        rr["cast"] += 1
                        eng = ("act", "dve")[rr["cast"] % 2]
                        cp(eng, sbv[:, ka:kb_, 0:ncols], sfv[:, ka:kb_, 0:ncols], (sf.b,), (sb_.b,))
            else:
                for k in range(kn):
                    rr["cast"] += 1
                    eng = ("act", "dve")[rr["cast"] % 2]
                    sc = gsb.v[:, k0 + k:k0 + k + 1]
                    if eng == "act":
                        act(sbv[:, k, 0:ncols], sfv[:, k, 0:ncols], AF.Copy, (sf.b, gsb.b), (sb_.b,), scale=sc)
                    else:
                        ts(eng, sbv[:, k, 0:ncols], sfv[:, k, 0:ncols], sc, None, ALU.mult, None,
                           (sf.b, gsb.b), (sb_.b,))
            dma(Wdst.ap()[blk, :, k0:k0 + kn, 0:ncols], sbv[:, 0:kn, 0:ncols], (sb_.b,), (DB(Wdst.name),),
                "%ss%d" % (self.pfx, i), q=self.store_q)

        def step(self):
            n = self.n
            if n >= len(self.items):
                return False
            if n == 0:
                for q in range(min(self.look, len(self.items))):
                    self.load(q)
            if n + self.look < len(self.items):
                self.load(n + self.look)
            self.cast_store(n)
            self.n += 1
            return True

    pc1 = Precast(16 if DC >= 16 else DC, 3, "p0")
    for bi, (kind, idx, c0, ncols) in enumerate(blocks):
        pc1.add(w_in, Wt_in, D, c0, ncols, bi, gmix_sb)
    while pc1.step():
        pass
    P.barrier()
    mem.release(base_mark)
    pc2 = Precast(8 if DC >= 8 else DC, 2, "p2")
    pc2.store_q = "pool"
    for b in range(D // 512):
        pc2.add(w_oa, Wt_oa, DA, b * 512, 512, b, None)
        pc2.add(w_ob, Wt_ob, DA, b * 512, 512, b, None)
        pc2.add(w_o, Wt_o, D, b * 512, 512, b, None)
    for b in range(DFF // 512):
        pc2.add(w_up, Wt_up, D, b * 512, 512, b, gmlp_sb)
    for b in range(D // 512):
        pc2.add(w_dn, Wt_dn, DFF, b * 512, 512, b, None)
    p2_per_iter = 1
    a_mark = mem.mark()

    def rstd_from_ss(i):
        ts("dve", rstd.v[:, i:i + 1], ss.v[:, i:i + 1], 1.0 / D, EPS, ALU.mult, ALU.add, (ss.b,), (rstd.b,))
        act(rstd.v[:, i:i + 1], rstd.v[:, i:i + 1], AF.Sqrt, (rstd.b,), (rstd.b,))
        P.op("dve", (lambda i_: (lambda e: e.reciprocal(rstd.v[:, i_:i_ + 1], rstd.v[:, i_:i_ + 1])))(i),
             (rstd.b,), (rstd.b,))

    def norm_transpose(src_tile, src_view, m_idx, xn_t, dstT, dst_b, col0, junk):
        P.op("dve", (lambda mi: (lambda e: e.memset(ss.v[:, mi:mi + 1], 0.0)))(m_idx), (), (ss.b,))
        act(junk.v, src_view, AF.Square, (src_tile.b,), (junk.b, ss.b), accum=ss.v[:, m_idx:m_idx + 1])
        rstd_from_ss(m_idx)
        ts("dve", xn_t.v, src_view, rstd.v[:, m_idx:m_idx + 1], None, ALU.mult, None,
           (src_tile.b, rstd.b), (xn_t.b,))
        for c4 in range(0, DC, 4):
            n4 = min(4, DC - c4)
            bk = nbank()
            bv = bk.v.bitcast(BF16)
            for q in range(n4):
                tr(bv[:, q * 128:(q + 1) * 128], xn_t.v[:, (c4 + q) * 128:(c4 + q + 1) * 128], ident.v,
                   (xn_t.b, ident.b), (bk.b,))
            cp(evac_eng(), dstT[:, c4:c4 + n4, col0:col0 + 128],
               bv[:, 0:n4 * 128].rearrange("p (q t) -> p q t", q=n4), (bk.b,), (dst_b,))

    xnT = bf(mem, DC * 512, "xnT")
    xnT3 = xnT.v.rearrange("p (c t) -> p c t", t=512)
    wsl = [bf(mem, DC * 512, "w%d" % i) for i in range(2)]
    xts = [f32(mem, D, "xt0")] * 2
    xns = [bf(mem, D, "xn0")] * 2
    junk = xns[0]
    stage = [bf(mem, 4 * 512, "stage%d" % i) for i in range(2)]
    stage_f = [f32(mem, 4 * 32, "stagef%d" % i) for i in range(2)]
    wcnt = [0]
    scnt = [0]

    def load_w(Wt, blk, kc0, kcn, ncols=512):
        i = wcnt[0] % 2
        wcnt[0] += 1
        w = wsl[i]
        wv = w.v.rearrange("p (k c) -> p k c", c=512)
        dma(wv[:, 0:kcn, 0:ncols], Wt.ap()[blk, :, kc0:kc0 + kcn, 0:ncols], (DB(Wt.name),), (w.b,), "w%d" % i)
        return w, wv

    def fm_group(wv, w, kcn, c_lo, c_n, rhs_fn, rhs_reads, ntok):
        bk = nbank()
        for kc in range(kcn):
            mm(bk.v[0:c_n, 0:ntok], wv[:, kc, c_lo:c_lo + c_n], rhs_fn(kc), kc == 0, kc == kcn - 1,
               (w.b,) + rhs_reads, (bk.b,))
        return bk

    for G in range(NG):
        for t4 in range(4):
            m = 4 * G + t4
            xt, xn = xts[m % 2], xns[m % 2]
            dma(xt.v, x_in[m * 128:(m + 1) * 128, :], (), (xt.b,), "xt%d" % (m % 2))
            norm_transpose(xt, xt.v, m, xn, xnT3, xnT.b, t4 * 128, junk)
        tsl = slice(G * 512, (G + 1) * 512)
        for bi, (kind, idx, c0, ncols) in enumerate(blocks):
            w, wv = load_w(Wt_in, bi, 0, DC, ncols)
            if kind in ("qa", "ka", "qb", "kb", "iq", "ga", "gb"):
                sg = stage[scnt[0] % 2]
                sgi = scnt[0] % 2
                scnt[0] += 1
                sgv = sg.v.rearrange("p (q t) -> p q t", t=512)
                nq = ncols // 128
                for q in range(nq):
                    bk = fm_group(wv, w, DC, q * 128, 128, lambda kc: xnT3[:, kc, :], (xnT.b,), 512)
                    if kind in ("ga", "gb"):
                        act(sgv[:, q, :], bk.v, AF.Sigmoid, (bk.b,), (sg.b,))
                    else:
                        cp(evac_eng(), sgv[:, q, :], bk.v, (bk.b,), (sg.b,))
                dst = {"qa": QaT, "qb": QbT, "ka": KaT_i, "kb": KbT_i, "iq": IqT, "ga": SgA, "gb": SgB}[kind]
                dma(dst.ap()[idx * 512:idx * 512 + ncols, tsl].rearrange("(q p) t -> p q t", p=128),
                    sgv[:, 0:nq, :], (sg.b,), (DB(dst.name),), "st%d" % sgi, q="pool")
            elif kind in ("va", "vb"):
                sg = stage[scnt[0] % 2]
                sgi = scnt[0] % 2
                scnt[0] += 1
                sgv = sg.v.rearrange("p (q t) -> p q t", t=512)
                for t4 in range(4):
                    bk = nbank()
                    for kc in range(DC):
                        mm(bk.v[:, 0:ncols], xnT3[:, kc, t4 * 128:(t4 + 1) * 128], wv[:, kc, 0:ncols],
                           kc == 0, kc == DC - 1, (w.b, xnT.b), (bk.b,))
                    cp(evac_eng(), sgv[:, t4, 0:ncols], bk.v[:, 0:ncols], (bk.b,), (sg.b,))
                dst = Va_i if kind == "va" else Vb_i
                dma(dst.ap()[G * 512:(G + 1) * 512, idx * 512:idx * 512 + ncols].rearrange("(q p) e -> p q e", p=128),
                    sgv[:, :, 0:ncols], (sg.b,), (DB(dst.name),), "st%d" % sgi, q="pool")
            elif kind == "misc":
                sg = stage[scnt[0] % 2]
                sgi = scnt[0] % 2
                scnt[0] += 1
                bk = fm_group(wv, w, DC, 0, 64, lambda kc: xnT3[:, kc, :], (xnT.b,), 512)
                cp(evac_eng(), sg.v[0:64, 0:512], bk.v[0:64, :], (bk.b,), (sg.b,))
                dma(Ik_i.ap()[:, tsl], sg.v[0:64, 0:512], (sg.b,), (DB("Ik_i"),), "st%d" % sgi, q="pool")
                for t4 in range(4):
                    m = 4 * G + t4
                    bk = nbank()
                    for kc in range(DC):
                        mm(bk.v[:, 0:HI], xnT3[:, kc, t4 * 128:(t4 + 1) * 128], wv[:, kc, 64:64 + HI],
                           kc == 0, kc == DC - 1, (w.b, xnT.b), (bk.b,))
                    ts("dve", wfin3[:, m, :], bk.v[:, 0:HI], (HI ** -0.5) * (64 ** -0.5), None, ALU.mult, None,
                       (bk.b,), (wfin.b,))
            elif kind == "fl":
                sf = stage_f[scnt[0] % 2]
                sfi = scnt[0] % 2
                scnt[0] += 1
                sfv = sf.v.rearrange("p (q h) -> p q h", h=32)
                for t4 in range(4):
                    bk = nbank()
                    for kc in range(DC):
                        mm(bk.v[:, 0:H], xnT3[:, kc, t4 * 128:(t4 + 1) * 128], wv[:, kc, 0:H],
                           kc == 0, kc == DC - 1, (w.b, xnT.b), (bk.b,))
                    cp("dve", sfv[:, t4, 0:H], bk.v[:, 0:H], (bk.b,), (sf.b,))
                dma(Fl_i.ap()[G * 512:(G + 1) * 512, :].rearrange("(q p) h -> p q h", p=128), sfv[:, :, 0:H],
                    (sf.b,), (DB("Fl_i"),), "sf%d" % sfi, q="pool")
            for _ in range(p2_per_iter):
                pc2.step()
    ts("dve", wabs.v, wfin.v, -1.0, None, ALU.mult, None, (wfin.b,), (wabs.b,))
    tt("dve", wabs.v, wabs.v, wfin.v, ALU.max, (wabs.b, wfin.b), (wabs.b,))
    ts("dve", wsgn.v, wfin.v, 0.0, 2.0, ALU.is_ge, ALU.mult, (wfin.b,), (wsgn.b,))
    ts("dve", wsgn.v, wsgn.v, -1.0, None, ALU.add, None, (wsgn.b,), (wsgn.b,))

    groups = [[0, 1, 2, 3], [4, 5, 6, 7]]

    def allgather(src_ap, dst_ap, sname, dname):
        P.op("pool", lambda e: e.collective_compute("AllGather", ALU.bypass, replica_groups=groups,
                                                    ins=[src_ap], outs=[dst_ap]),
             (DB(sname),), (DB(dname),), dma_key="cc", inc_amt=1)

    allgather(Ik_i.ap(), Ik_g.ap(), "Ik_i", "Ik_g")
    allgather(Fl_i.ap(), Fl_g.ap(), "Fl_i", "Fl_g")
    for (src, dst, nslab) in ((KaT_i, KaT_g, H), (Va_i, Va_g, NT), (KbT_i, KbT_g, H), (Vb_i, Vb_g, NT)):
        for i in range(nslab):
            allgather(src.ap()[i * 128:(i + 1) * 128, :], dst.ap()[i * 512:(i + 1) * 512, :], src.name, dst.name)

    P.barrier(skip_keys=("cc",))
    mem.release(base_mark)
    pc3 = Precast(pc2.KH, 5, "p3")
    pc3.items = pc2.items[pc2.n:]
    while pc3.step():
        pass
    P.barrier(skip_keys=("cc",))
    mem.release(base_mark)

    L = f32(mem, NTA * H, "L")
    L3 = L.v.rearrange("p (n h) -> p n h", h=H)
    totbc = f32(mem, NTA * H, "totbc")
    tot3 = totbc.v.rearrange("p (n h) -> p n h", h=H)
    carry = f32(mem, NTA * H, "carry")
    car3 = carry.v.rearrange("p (n h) -> p n h", h=H)
    for r in range(4):
        dma(L3[:, r * NT:(r + 1) * NT, :],
            Fl_g.ap()[r * TL:(r + 1) * TL, :].rearrange("(n p) h -> p n h", p=128), (DB("Fl_g"),), (L.b,), "c0")
    for n in range(NTA):
        tt("dve", L3[:, n, :], L3[:, n, :], bf_bc.v, ALU.add, (L.b, bf_bc.b), (L.b,))
    act(L.v, L.v, AF.Exp, (L.b,), (L.b,), scale=-1.0)
    act(L.v, L.v, AF.Ln, (L.b,), (L.b,), bias=1.0)
    NF = NTA * H
    for c0 in range(0, NF, 512):
        cn = min(512, NF - c0)
        bk = nbank()
        mm(bk.v[:, 0:cn], tri_f.v, L.v[:, c0:c0 + cn], True, True, (tri_f.b, L.b), (bk.b,))
        cp("dve", Cpos.v[:, c0:c0 + cn], bk.v[:, 0:cn], (bk.b,), (Cpos.b,))
        bk = nbank()
        mm(bk.v[:, 0:cn], ones_f.v, L.v[:, c0:c0 + cn], True, True, (ones_f.b, L.b), (bk.b,))
        cp("dve", totbc.v[:, c0:c0 + cn], bk.v[:, 0:cn], (bk.b,), (totbc.b,))

    def nidx(g):
        return (g % 4) * NT + g // 4

    P.op("dve", lambda e: e.memset(carry.v, 0.0), (), (carry.b,))
    for g in range(1, NTG):
        tt("dve", car3[:, nidx(g), :], car3[:, nidx(g - 1), :], tot3[:, nidx(g - 1), :], ALU.add,
           (carry.b, totbc.b), (carry.b,))
    tt("dve", Cpos.v, Cpos.v, carry.v, ALU.add, (Cpos.b, carry.b), (Cpos.b,))
    stt("dve", totbc.v, totbc.v, 0.5, carry.v, ALU.mult, ALU.add, (totbc.b, carry.b), (totbc.b,))
    for r in range(4):
        src = tot3[:, r * NT:(r + 1) * NT, :]
        if r == 0:
            ts("dve", Cref3, src, sel_sb.v[:, 0:1], None, ALU.mult, None, (totbc.b, sel_sb.b), (Cref.b,))
        else:
            stt("dve", Cref3, src, sel_sb.v[:, r:r + 1], Cref3, ALU.mult, ALU.add,
                (totbc.b, sel_sb.b, Cref.b), (Cref.b,))
    if debug:
        dma(dbg["cpos"].ap(), Cpos3, (Cpos.b,), (DB("dbgc"),), "dbgs")
    P.barrier(skip_keys=("cc",))
    mem.release(base_mark)

    maskT = Tl(mem.alloc_top(NTG * 512 * 2, BF16), "maskT")
    maskT3 = maskT.v.rearrange("p (k t) -> p k t", t=512)
    grp_mark = mem.mark()
    SCALE = 128 ** -0.5

    for G in range(NG):
        NKT = 4 * (4 * G + 4)
        ikT = bf(mem, 4 * TL, "ikT")
        ikT3 = ikT.v.rearrange("p (r t) -> p r t", r=4)
        dma(ikT3[0:64], Ik_g.ap().rearrange("(r d) t -> d r t", d=64), (DB("Ik_g"),), (ikT.b,), "c0")
        scores = [f32(mem, NTG * 128, "score%d" % i) for i in range(2)]
        m01s = [bf(mem, 512, "m01_%d" % i) for i in range(2)]
        iqs = [bf(mem, HI * 128, "iq%d" % i) for i in range(2)]
        rl = [bf(mem, 512, "rl%d" % i) for i in range(6)]
        dg = bf(mem, HI * 128, "dg")
        dg3 = dg.v.rearrange("p (h t) -> p h t", t=128)
        mx8s = [f32(mem, 8, "mx8_%d" % i) for i in range(2)]
        rcnt_box = [0]
        def score_phase(t4):
            m = 4 * G + t4
            score = scores[m % 2]
            iq = iqs[m % 2]
            iq3 = iq.v.rearrange("p (h t) -> p h t", t=128)
            dma(iq3[0:64], IqT.ap().rearrange("(h d) t -> d h t", d=64)[:, :, m * 128:(m + 1) * 128],
                (DB("IqT"),), (iq.b,), "iq%d" % (m % 2))
            for h in range(HI):
                ts("pool", dg3[:, h, :], ident.v, wsgn3[:, m, h:h + 1], None, ALU.mult, None,
                   (ident.b, wsgn.b), (dg.b,))
            seq = [(kb, h) for kb in range(m + 1) for h in range(HI)]
            rls = [None] * len(seq)

            def s1(i):
                kb, h = seq[i]
                bk = nbank(0, 6)
                mm(bk.v.rearrange("p (r s) -> p r s", r=4), iq3[0:64, h, :],
                   ikT3[0:64, :, kb * 128:(kb + 1) * 128], True, True, (iq.b, ikT.b), (bk.b,))
                r_ = rl[rcnt_box[0] % 6]
                rcnt_box[0] += 1
                rls[i] = r_
                act(r_.v, bk.v, AF.Relu, (bk.b, wabs.b), (r_.b,), scale=wabs3[:, m, h:h + 1])

            def s2(i):
                kb, h = seq[i]
                accb = banks[6 + kb % 2]
                mm(accb.v, dg3[:, h, :], rls[i].v, h == 0, h == HI - 1, (dg.b, rls[i].b), (accb.b,))
                if h == HI - 1:
                    sl = slice(kb * 512, (kb + 1) * 512)
                    cp("act", score.v[:, sl], accb.v, (accb.b,), (score.b,))

            LA1 = 3
            for i in range(len(seq) + LA1):
                if i < len(seq):
                    s1(i)
                if i >= LA1:
                    s2(i - LA1)

        def topk_phase(t4):
            m = 4 * G + t4
            score = scores[m % 2]
            mx8 = mx8s[m % 2]
            nk = 512 * (m + 1)
            dsl = slice(m * 512, (m + 1) * 512)
            tt("dve", score.v[:, dsl], score.v[:, dsl], negv.v, ALU.add, (score.b, negv.b), (score.b,))
            for it in range(cfg.TOPK // 8):
                P.op("dve", (lambda nk_, sc_, mx_: (lambda e: e.max(out=mx_.v, in_=sc_.v[:, 0:nk_])))(nk, score, mx8),
                     (score.b,), (mx8.b,))
                P.op("dve", (lambda nk_, sc_, mx_: (lambda e: e.match_replace(
                    out=sc_.v[:, 0:nk_], in_to_replace=mx_.v, in_values=sc_.v[:, 0:nk_], imm_value=REPL)))(nk, score, mx8),
                     (score.b, mx8.b), (score.b,))
            for k4 in range(0, 4 * (m + 1), 4):
                m01 = m01s[(k4 // 4) % 2]
                P.op("dve", (lambda mo, k4_, sc_: (lambda e: e.tensor_single_scalar(
                    mo.v, sc_.v[:, k4_ * 128:(k4_ + 4) * 128], REPL, ALU.is_equal)))(m01, k4, score),
                    (score.b,), (m01.b,))
                if k4 == 4 * m:
                    tt("dve", m01.v, m01.v, valid.v, ALU.mult, (m01.b, valid.b), (m01.b,))
                bk = nbank(0, 6)
                bv = bk.v.bitcast(BF16)
                for q in range(4):
                    tr(bv[:, q * 128:(q + 1) * 128], m01.v[:, q * 128:(q + 1) * 128], ident.v,
                       (m01.b, ident.b), (bk.b,))
                cp("act", maskT3[:, k4:k4 + 4, t4 * 128:(t4 + 1) * 128],
                   bv[:, 0:512].rearrange("p (q t) -> p q t", q=4), (bk.b,), (maskT.b,))

        score_phase(0)
        for t4 in range(4):
            if t4 + 1 < 4:
                score_phase(t4 + 1)
            topk_phase(t4)
        if debug:
            dma(dbg["mask"].ap()[G, :, 0:NKT, :], maskT3[:, 0:NKT, :], (maskT.b,), (DB("dbgm"),), "dbgs")
        P.barrier(skip_keys=("cc",) if G == 0 else ())
        mem.release(grp_mark)
        yaT = bf(mem, H * 512, "yaT"); ybT = bf(mem, H * 512, "ybT")
        yaT3 = yaT.v.rearrange("p (h t) -> p h t", t=512)
        ybT3 = ybT.v.rearrange("p (h t) -> p h t", t=512)
        yab_mark = mem.mark()

        qs = [bf(mem, 512, "q%d" % i) for i in range(2)]
        kts = [bf(mem, 4 * TL, "kt%d" % i) for i in range(2)]
        vs = [bf(mem, 4 * NT * 128, "v%d" % i) for i in range(2)]
        ebf = [f32(mem, 12 * 128, "ebf0")] * 2
        ebs = [bf(mem, 12 * 128, "eb%d" % i) for i in range(2)]
        pts = [bf(mem, 512, "pt%d" % i) for i in range(4)]
        Bh = [f32(mem, 4 * NTA, "Bh%d" % i) for i in range(2)]
        rden = f32(mem, 512, "rden")
        accPs = [f32(mem, 512, "accP%d" % i) for i in range(2)]
        pcnt_box = [0]
        hcnt = 0
        MK = 4 * G + 4
        for branch in ("A", "B"):
            QT = QaT if branch == "A" else QbT
            KTg = KaT_g if branch == "A" else KbT_g
            Vg = Va_g if branch == "A" else Vb_g
            yT3, yT = (yaT3, yaT) if branch == "A" else (ybT3, ybT)
            for h in range(H):
                i2 = hcnt % 2
                hcnt += 1
                q_, kt, v_ = qs[i2], kts[i2], vs[i2]
                kt3 = kt.v.rearrange("p (r t) -> p r t", r=4)
                v4 = v_.v.rearrange("p (r m d) -> p r m d", r=4, d=128)
                dma(q_.v, QT.ap()[h * 128:(h + 1) * 128, G * 512:(G + 1) * 512], (DB(QT.name),), (q_.b,), "q%d" % i2)
                dma(kt3[:, :, 0:MK * 128],
                    KTg.ap().rearrange("(h r d) t -> d h r t", r=4, h=H)[:, h, :, 0:MK * 128],
                    (DB(KTg.name),), (kt.b,), "kt%d" % i2)
                for r in range(4):
                    dma(v4[:, r, 0:MK, :],
                        Vg.ap().rearrange("(m r p) e -> p m r e", r=4, p=128)[:, 0:MK, r, h * 128:(h + 1) * 128],
                        (DB(Vg.name),), (v_.b,), "v%d" % i2)
                if branch == "A":
                    e_f, e_b = ebf[i2], ebs[i2]
                    dma(e_f.v.rearrange("p (a r t) -> p a r t", a=3, r=4), EB[h], (), (e_f.b,), "ebl")
                    act(e_b.v, e_f.v, AF.Exp, (e_f.b,), (e_b.b,))
                    e4 = e_b.v.rearrange("p (a r t) -> p a r t", a=3, r=4)
                else:
                    bh = Bh[i2]
                    bh3 = bh.v.rearrange("p (t n) -> p t n", t=4)
                    for t4 in range(4):
                        m = 4 * G + t4
                        ts("dve", bh3[:, t4, :], Cpos3[:, :, h], Cref3[:, m, h:h + 1], 80.0, ALU.subtract, ALU.min,
                           (Cpos.b, Cref.b), (bh.b,))
                bO, bD = (banks[4], banks[5]) if i2 == 0 else (banks[6], banks[7])
                tiles = [(mk, r) for mk in range(MK) for r in range(4)]
                nTl = len(tiles)
                ptl = [None] * nTl

                def stage1(i):
                    mk, r = tiles[i]
                    a0 = max(0, mk - 4 * G)
                    c0 = a0 * 128
                    kt_idx = 4 * mk + r
                    bk = nbank(0, 4)
                    mm(bk.v[:, c0:512], kt3[:, r, mk * 128:(mk + 1) * 128], q_.v[:, c0:512], True, True,
                       (kt.b, q_.b), (bk.b,))
                    pt = pts[pcnt_box[0] % 4]
                    pcnt_box[0] += 1
                    ptl[i] = pt
                    if branch == "A":
                        act(pt.v[:, c0:512], bk.v[:, c0:512], AF.Exp, (bk.b,), (pt.b,), scale=SCALE)
                        tf = max(a0, mk + 3 - 4 * G)
                        if tf < 4:
                            cs = slice(tf * 128, 512)
                            stt("dve", pt.v[:, cs], pt.v[:, cs], erb15.v[:, h:h + 1], maskT3[:, kt_idx, cs],
                                ALU.mult, ALU.mult, (pt.b, erb15.b, maskT.b), (pt.b,))
                        for t4 in range(a0, min(tf, 4)):
                            dm_ = 4 * G + t4 - mk
                            cs = slice(t4 * 128, (t4 + 1) * 128)
                            tt("dve", pt.v[:, cs], pt.v[:, cs], maskT3[:, kt_idx, cs], ALU.mult,
                               (pt.b, maskT.b), (pt.b,))
                            tt("pool", pt.v[:, cs], pt.v[:, cs], e4[:, dm_, r, :], ALU.mult,
                               (pt.b, e_b.b), (pt.b,))
                    else:
                        n_ = r * NT + mk
                        for t4 in range(a0, 4):
                            cs = slice(t4 * 128, (t4 + 1) * 128)
                            act(pt.v[:, cs], bk.v[:, cs], AF.Exp, (bk.b, bh.b), (pt.b,), scale=SCALE,
                                bias=bh3[:, t4, n_:n_ + 1])
                            if 4 * G + t4 == mk:
                                tt("pool", pt.v[:, cs], pt.v[:, cs], cm.v[:, r * 128:(r + 1) * 128], ALU.mult,
                                   (pt.b, cm.b), (pt.b,))

                def stage2(i):
                    mk, r = tiles[i]
                    c0 = max(0, mk - 4 * G) * 128
                    pt = ptl[i]
                    mm(bO.v[:, c0:512], v4[:, r, mk, :], pt.v[:, c0:512], i == 0, i == nTl - 1, (v_.b, pt.b), (bO.b,))
                    if branch == "A":
                        mm(bD.v[:, c0:512], ones_bf.v, pt.v[:, c0:512], i == 0, i == nTl - 1,
                           (ones_bf.b, pt.b), (bD.b,))
                    else:
                        accP = accPs[i2]
                        if i == 0:
                            cp("dve", accP.v, pt.v, (pt.b,), (accP.b,))
                        else:
                            tt("dve", accP.v[:, c0:512], accP.v[:, c0:512], pt.v[:, c0:512], ALU.add,
                               (accP.b, pt.b), (accP.b,))
                        if i == nTl - 1:
                            mm(bD.v, ones_f.v, accP.v, True, True, (ones_f.b, accP.b), (bD.b,))

                LA = 2
                for i in range(nTl + LA):
                    if i < nTl:
                        stage1(i)
                    if i >= LA:
                        stage2(i - LA)
                P.op("dve", (lambda rd, bd: (lambda e: e.reciprocal(rd.v, bd.v)))(rden, bD), (bD.b,), (rden.b,))
                tt("dve", yT3[:, h, :], bO.v, rden.v, ALU.mult, (bO.b, rden.b), (yT.b,))
        if debug:
            dma(dbg["ya"].ap()[G], yaT3, (yaT.b,), (DB("dbgya"),), "dbgs")
            dma(dbg["yb"].ap()[G], ybT3, (ybT.b,), (DB("dbgyb"),), "dbgs")
        P.barrier()
        mem.release(yab_mark)

        TT = 256
        top_size = mem.size
        mem.size = SB_BYTES
        wsl2 = [bf(mem, 32 * 512, "wc%d" % i) for i in range(2)]
        x1 = f32(mem, 2 * D, "x1")
        x13 = x1.v.rearrange("p (t d) -> p t d", t=2)
        mh = bf(mem, DC * TT, "mh")
        mh3 = mh.v.rearrange("p (c t) -> p c t", t=TT)
        FQ = min(32, FC)
        hT = bf(mem, FQ * TT, "hT")
        hT3 = hT.v.rearrange("p (c t) -> p c t", t=TT)
        xn2 = bf(mem, D, "xn2")
        junk2 = xn2
        assert FQ * TT * 2 >= 0
        gf_t = f32(mem, D, "gfin_bc") if FQ * TT * 2 < D * 4 else None
        gfv = gf_t.v if gf_t is not None else hT.v.bitcast(F32)[:, 0:D]
        gfb = gf_t.b if gf_t is not None else hT.b
        sgs = [bf(mem, 2 * TT, "sg%d" % i) for i in range(2)]
        tA = f32(mem, TT, "tA"); tB = f32(mem, TT, "tB")
        rtmp = [f32(mem, TT, "rtmp%d" % i) for i in range(2)]
        wc = [0]

        def load_w2(Wt, blk, kc0, kcn):
            i = wc[0] % 2
            wc[0] += 1
            w = wsl2[i]
            wv = w.v.rearrange("p (k c) -> p k c", c=512)
            dma(wv[:, 0:kcn, :], Wt.ap()[blk, :, kc0:kc0 + kcn, :], (DB(Wt.name),), (w.b,), "wc%d" % i)
            return w, wv

        for sub in range(2):
            tok0 = G * 512 + sub * TT
            ysl = slice(sub * TT, (sub + 1) * TT)
            dma(x13, x_in[tok0:tok0 + TT, :].rearrange("(t p) d -> p t d", p=128), (), (x1.b,), "x1")
            for nb in range(D // 512):
                wa, wav = load_w2(Wt_oa, nb, 0, H)
                wb, wbv = load_w2(Wt_ob, nb, 0, H)
                for q in range(4):
                    dmc = nb * 4 + q
                    sg = sgs[dmc % 2]
                    sgv = sg.v.rearrange("p (a t) -> p a t", a=2)
                    dma(sgv[:, 0, :], SgA.ap()[dmc * 128:(dmc + 1) * 128, tok0:tok0 + TT], (DB("SgA"),), (sg.b,),
                        "sg%d" % (dmc % 2))
                    dma(sgv[:, 1, :], SgB.ap()[dmc * 128:(dmc + 1) * 128, tok0:tok0 + TT], (DB("SgB"),), (sg.b,),
                        "sg%d" % (dmc % 2))
                    bA = fm_group(wav, wa, H, q * 128, 128, lambda kc: yaT3[:, kc, ysl], (yaT.b,), TT)
                    bB = fm_group(wbv, wb, H, q * 128, 128, lambda kc: ybT3[:, kc, ysl], (ybT.b,), TT)
                    tt("dve", tA.v, bA.v[:, 0:TT], sgv[:, 0, :], ALU.mult, (bA.b, sg.b), (tA.b,))
                    tt("dve", tB.v, bB.v[:, 0:TT], sgv[:, 1, :], ALU.mult, (bB.b, sg.b), (tB.b,))
                    tt("pool", mh3[:, dmc, :], tA.v, tB.v, ALU.add, (tA.b, tB.b), (mh.b,))
            for nb in range(D // 512):
                w, wv = load_w2(Wt_o, nb, 0, DC)
                for t2 in range(2):
                    bk = nbank()
                    for kc in range(DC):
                        mm(bk.v, mh3[:, kc, t2 * 128:(t2 + 1) * 128], wv[:, kc, :], kc == 0, kc == DC - 1,
                           (w.b, mh.b), (bk.b,))
                    tt("dve", x13[:, t2, nb * 512:(nb + 1) * 512], x13[:, t2, nb * 512:(nb + 1) * 512], bk.v, ALU.add,
                       (bk.b, x1.b), (x1.b,))
            for t2 in range(2):
                norm_transpose(x1, x13[:, t2, :], 0, xn2, mh3, mh.b, t2 * 128, junk2)
            for fq in range(0, FC, FQ):
                for fb in range(FQ // 4):
                    w, wv = load_w2(Wt_up, (fq // 4) + fb, 0, DC)
                    for q in range(4):
                        bk = fm_group(wv, w, DC, q * 128, 128, lambda kc: mh3[:, kc, :], (mh.b,), TT)
                        r_ = rtmp[(fb * 4 + q) % 2]
                        act(r_.v, bk.v[:, 0:TT], AF.Relu, (bk.b,), (r_.b,))
                        tt("pool", hT3[:, fb * 4 + q, :], r_.v, r_.v, ALU.mult, (r_.b,), (hT.b,))
                for nb in range(D // 512):
                    w, wv = load_w2(Wt_dn, nb, fq, FQ)
                    for t2 in range(2):
                        bk = nbank()
                        for fc in range(FQ):
                            mm(bk.v, hT3[:, fc, t2 * 128:(t2 + 1) * 128], wv[:, fc, :], fc == 0, fc == FQ - 1,
                               (w.b, hT.b), (bk.b,))
                        tt("dve", x13[:, t2, nb * 512:(nb + 1) * 512], x13[:, t2, nb * 512:(nb + 1) * 512], bk.v,
                           ALU.add, (bk.b, x1.b), (x1.b,))
            dma(gfv, gfin.partition_broadcast(128), (), (gfb,), "c1")
            for t2 in range(2):
                P.op("dve", lambda e: e.memset(ss.v[:, 1:2], 0.0), (), (ss.b,))
                act(junk2.v, x13[:, t2, :], AF.Square, (x1.b,), (junk2.b, ss.b), accum=ss.v[:, 1:2])
                rstd_from_ss(1)
                stt("dve", x13[:, t2, :], x13[:, t2, :], rstd.v[:, 1:2], gfv, ALU.mult, ALU.mult,
                    (x1.b, rstd.b, gfb), (x1.b,))
            dma(out[tok0:tok0 + TT, :].rearrange("(t p) d -> p t d", p=128), x13, (x1.b,), (DB("out"),), "x1o",
                q="pool")
        P.barrier()
        mem.release(grp_mark)
        mem.size = top_size

    P.emit(nc, st)
    st.close()
    return nc


def t5_bucket_np(rel, nbk=32, maxd=1024):
    half = nbk // 2
    max_exact = half // 2
    base = np.where(rel > 0, half, 0)
    n = np.abs(rel)
    nf = np.maximum(n, max_exact).astype(np.float32)
    large = max_exact + (np.log(nf / np.float32(max_exact)) / np.float32(math.log(maxd / max_exact))
                         * np.float32(half - max_exact)).astype(np.int32)
    large = np.minimum(large, half - 1)
    return base + np.where(n < max_exact, n, large)


def core_consts(cfg, j, rel_bias):
    H = cfg.H
    p = np.arange(128)
    sel = np.zeros((1, 4), np.float32); sel[0, j] = 1.0
    negv = np.zeros((128, 4, 128), np.float32)
    valid = np.zeros((128, 4, 128), np.float32)
    cm = np.zeros((128, 4, 128), np.float32)
    for r in range(4):
        kpos = r * 128 + p[None, :]
        qpos = j * 128 + p[:, None]
        vis_chunk = kpos < (qpos // 64 + 1) * 64
        valid[:, r, :] = vis_chunk
        negv[:, r, :] = np.where(vis_chunk, 0.0, NEGFILL)
        cm[:, r, :] = ((r * 128 + p[:, None]) <= (j * 128 + p[None, :]))
    eb = np.zeros((H, 128, 3, 4, 128), np.float32)
    for dm in range(3):
        for r in range(4):
            delta = -4 * dm + r - j
            rel = delta * 128 + p[:, None] - p[None, :]
            bk = t5_bucket_np(rel, cfg.NBK, cfg.MAXD)
            eb[:, :, dm, r, :] = np.transpose(rel_bias[bk], (2, 0, 1))
    return sel, negv, valid, cm, eb


def make_in_maps(cfg, inputs):
    x = np.asarray(inputs["x"], np.float32)
    rel_bias = np.asarray(inputs["rel_bias"], np.float32)
    DC = cfg.DC
    shared = {
        "w_in": np.ascontiguousarray(inputs["w_in"][0]),
        "w_out_a": np.ascontiguousarray(inputs["w_out_a"][0]),
        "w_out_b": np.ascontiguousarray(inputs["w_out_b"][0]),
        "w_o": np.ascontiguousarray(inputs["w_o"][0]),
        "w_up": np.ascontiguousarray(inputs["w_up"][0]),
        "w_down": np.ascontiguousarray(inputs["w_down"][0]),
        "b_f": np.ascontiguousarray(inputs["b_f"][0:1]),
        "g_mixT": np.ascontiguousarray(np.asarray(inputs["g_mix"][0]).reshape(DC, 128).T),
        "g_mlpT": np.ascontiguousarray(np.asarray(inputs["g_mlp"][0]).reshape(DC, 128).T),
        "g_final": np.ascontiguousarray(inputs["g_final"]),
        "rb15": np.ascontiguousarray(rel_bias[15:16, :]),
    }
    maps = []
    for c in range(8):
        b, j = c // 4, c % 4
        xt = x[b].reshape(cfg.NT, 4, 128, cfg.D)[:, j]
        sel, negv, valid, cm, eb = core_consts(cfg, j, rel_bias)
        d = dict(shared)
        d.update({"x": np.ascontiguousarray(xt.reshape(cfg.TL, cfg.D)), "EB": eb, "sel": sel, "negv": negv,
                  "valid": valid, "cm": cm})
        maps.append(d)
    return maps


def assemble(cfg, results):
    out = np.zeros((2, cfg.NT, 4, 128, cfg.D), np.float32)
    for c in range(8):
        b, j = c // 4, c % 4
        out[b, :, j] = np.asarray(results[c]["out"]).reshape(cfg.NT, 128, cfg.D)
    return out.reshape(2, cfg.S, cfg.D)


_NC_CACHE = {}


def kernel(**inputs):
    cfg = Cfg()
    if "nc" not in _NC_CACHE:
        _NC_CACHE["nc"] = build(cfg)
    nc = _NC_CACHE["nc"]
    maps = make_in_maps(cfg, {k: np.asarray(v) for k, v in inputs.items()})
    res = run_bass_kernel_spmd(nc, maps, core_ids=list(range(8)))
    return assemble(cfg, res.results)
```

```python
import math
from contextlib import ExitStack
import numpy as np
import ml_dtypes
import concourse.bass as bass
import concourse.mybir as mybir
from concourse.bass_utils import run_bass_kernel_spmd

F32 = mybir.dt.float32
BF16 = mybir.dt.bfloat16
U8 = mybir.dt.uint8
ALU = mybir.AluOpType
AF = mybir.ActivationFunctionType

REPL = -float(2.0 ** 100)
NEGFILL = -float(2.0 ** 101)
EPS = 1e-6
SAME_ENG_WAIT = True
NO_SELF_WAIT = ("pe", "act")


class Cfg:
    def __init__(self, S=8192, D=4096, H=16, HI=32, DFF=16384, TOPK=256):
        self.S, self.D, self.H, self.HI, self.DFF, self.TOPK = S, D, H, HI, DFF, TOPK
        self.DH, self.DI, self.NBK, self.MAXD = 128, 64, 32, 1024
        self.DA = H * 128
        self.NTG = S // 128
        self.NT = self.NTG // 4
        self.NG = self.NT // 4
        self.TL = self.NT * 128
        self.DC = D // 128
        self.FC = DFF // 128
        self.DIN = 3 * self.DA + HI * 64 + 64 + HI + 3 * self.DA + H + 2 * D


class Buf:
    __slots__ = ("name", "w", "r")

    def __init__(self, name=""):
        self.name, self.w, self.r = name, None, {}


class Op:
    __slots__ = ("eng", "fn", "deps", "dma_key", "needs_inc", "inc_val", "gid", "is_dma", "inc_amt")


def _dom(o):
    return ("dma", o.dma_key) if o.is_dma else ("eng", o.eng)


class Prog:
    ENGS = ("pe", "act", "dve", "pool", "sp")

    def __init__(self):
        self.streams = {e: [] for e in self.ENGS}
        self.dma_cnt = {}
        self.dma_last = {}
        self.gid = 0
        self.pending = {e: [] for e in self.ENGS}

    def op(self, eng, fn, reads=(), writes=(), dma_key=None, inc_amt=None):
        o = Op()
        o.eng, o.fn, o.dma_key = eng, fn, dma_key
        o.is_dma = dma_key is not None
        o.needs_inc = o.is_dma
        o.gid = self.gid
        self.gid += 1
        deps = {}

        def add(d):
            if d is None:
                return
            k = _dom(d)
            c = deps.get(k)
            if c is None or d.gid > c.gid:
                deps[k] = d

        for b in reads:
            add(b.w)
        for b in writes:
            add(b.w)
            for d in b.r.values():
                add(d)
        for d in self.pending[eng]:
            add(d)
        self.pending[eng] = []
        for k_, d in deps.items():
            if k_ == ("eng", eng) and (eng in NO_SELF_WAIT or not SAME_ENG_WAIT):
                continue
            d.needs_inc = True
        o.deps = deps
        if o.is_dma:
            o.inc_amt = 16 if inc_amt is None else inc_amt
            self.dma_cnt[dma_key] = self.dma_cnt.get(dma_key, 0) + o.inc_amt
            o.inc_val = self.dma_cnt[dma_key]
            self.dma_last[dma_key] = o
        k = _dom(o)
        for b in reads:
            b.r[k] = o
        for b in writes:
            b.w = o
            b.r = {}
        self.streams[eng].append(o)
        return o

    def barrier(self, skip_keys=()):
        lasts = []
        for st_ in self.streams.values():
            for o in reversed(st_):
                if not (o.is_dma and o.dma_key in skip_keys):
                    lasts.append(o)
                    break
        lasts += [o for k, o in self.dma_last.items() if k not in skip_keys]
        for e in self.ENGS:
            self.pending[e] = list(lasts)

    def emit(self, nc, st):
        for e in self.ENGS:
            c = 0
            for o in self.streams[e]:
                if not o.is_dma and o.needs_inc:
                    c += 1
                    o.inc_val = c
        sems = {}
        for e in self.ENGS:
            sems[("eng", e)] = st.enter_context(nc.semaphore("s_" + e))
        for i, k in enumerate(self.dma_cnt):
            sems[("dma", k)] = st.enter_context(nc.semaphore("d%d" % i))
        block = st.enter_context(nc.Block())
        prog = self

        def mk(e, final):
            def body(eng):
                known = {}
                for o in prog.streams[e]:
                    for k, d in o.deps.items():
                        if k == ("eng", e) and (e in NO_SELF_WAIT or not SAME_ENG_WAIT):
                            continue
                        v = d.inc_val
                        if known.get(k, 0) >= v:
                            continue
                        eng.wait_ge(sems[k], v)
                        known[k] = v
                    ins = o.fn(eng)
                    if o.needs_inc:
                        ins.then_inc(sems[_dom(o)], o.inc_amt if o.is_dma else 1)
                if final:
                    for k, v in prog.dma_cnt.items():
                        if known.get(("dma", k), 0) < v:
                            eng.wait_ge(sems[("dma", k)], v)
            return body

        block.sync(mk("sp", True))
        block.scalar(mk("act", False))
        block.vector(mk("dve", False))
        block.gpsimd(mk("pool", False))
        block.tensor(mk("pe", False))


class Mem:
    def __init__(self, big, size):
        self.big, self.size, self.off = big, size, 0

    def alloc(self, nbytes, dtype, shape=None):
        rb = (nbytes + 63) // 64 * 64
        assert self.off + rb <= self.size, ("SBUF overflow", self.off, rb, self.size)
        v = self.big[:, self.off:self.off + nbytes].bitcast(dtype)
        self.off += rb
        return v

    def alloc_top(self, nbytes, dtype):
        rb = (nbytes + 63) // 64 * 64
        self.size -= rb
        assert self.off <= self.size
        return self.big[:, self.size:self.size + nbytes].bitcast(dtype)

    def mark(self):
        return self.off

    def release(self, m):
        self.off = m


class Tl:
    __slots__ = ("v", "b")

    def __init__(self, v, name=""):
        self.v, self.b = v, Buf(name)


def bf(mem, n, name=""):
    return Tl(mem.alloc(n * 2, BF16), name)


def f32(mem, n, name=""):
    return Tl(mem.alloc(n * 4, F32), name)


def build(cfg, debug=False):
    S, D, H, HI, DFF = cfg.S, cfg.D, cfg.H, cfg.HI, cfg.DFF
    DA, NT, NG, TL, DC, FC, NTG = cfg.DA, cfg.NT, cfg.NG, cfg.TL, cfg.DC, cfg.FC, cfg.NTG
    NTA = 4 * NT
    nc = bass.Bass("TRN2", target_bir_lowering=False)
    P = Prog()

    def din(name, shape, dt=F32):
        return nc.dram_tensor(name, list(shape), dt, kind="ExternalInput").ap()

    def dscr(name, shape, dt=BF16):
        kind = "ExternalOutput" if debug else "Internal"
        return nc.dram_tensor(name, list(shape), dt, kind=kind)

    x_in = din("x", [TL, D])
    w_in = din("w_in", [D, cfg.DIN])
    w_oa = din("w_out_a", [DA, D])
    w_ob = din("w_out_b", [DA, D])
    w_o = din("w_o", [D, D])
    w_up = din("w_up", [D, DFF])
    w_dn = din("w_down", [DFF, D])
    bfv = din("b_f", [1, H])
    gmixT = din("g_mixT", [128, DC])
    gmlpT = din("g_mlpT", [128, DC])
    gfin = din("g_final", [D])
    rb15 = din("rb15", [1, H])
    EB = din("EB", [H, 128, 3, 4, 128])
    sel_in = din("sel", [1, 4])
    negv_in = din("negv", [128, 4, 128])
    valid_in = din("valid", [128, 4, 128])
    cm_in = din("cm", [128, 4, 128])
    out = nc.dram_tensor("out", [TL, D], F32, kind="ExternalOutput").ap()

    o_qa, o_ka, o_va = 0, DA, 2 * DA
    o_iq = 3 * DA
    o_ik = o_iq + HI * 64
    o_iw = o_ik + 64
    o_qb = o_iw + HI
    o_kb, o_vb = o_qb + DA, o_qb + 2 * DA
    o_fl = o_qb + 3 * DA
    o_ga = o_fl + H
    o_gb = o_ga + D
    blocks = []

    def addblk(kind, c0, total):
        for i in range(0, total, 512):
            blocks.append((kind, i // 512, c0 + i, min(512, total - i)))

    addblk("qa", o_qa, DA); addblk("ka", o_ka, DA); addblk("va", o_va, DA)
    addblk("iq", o_iq, HI * 64); blocks.append(("misc", 0, o_ik, 64 + HI))
    addblk("qb", o_qb, DA); addblk("kb", o_kb, DA); addblk("vb", o_vb, DA)
    blocks.append(("fl", 0, o_fl, H))
    addblk("ga", o_ga, D); addblk("gb", o_gb, D)
    NBI = len(blocks)

    Wt_in = dscr("Wt_in", [NBI, 128, DC, 512])
    Wt_oa = dscr("Wt_oa", [D // 512, 128, H, 512])
    Wt_ob = dscr("Wt_ob", [D // 512, 128, H, 512])
    Wt_o = dscr("Wt_o", [D // 512, 128, DC, 512])
    Wt_up = dscr("Wt_up", [DFF // 512, 128, DC, 512])
    Wt_dn = dscr("Wt_dn", [D // 512, 128, FC, 512])
    QaT = dscr("QaT", [DA, TL]); QbT = dscr("QbT", [DA, TL])
    IqT = dscr("IqT", [HI * 64, TL])
    SgA = dscr("SgA", [D, TL]); SgB = dscr("SgB", [D, TL])
    KaT_i = nc.dram_tensor("KaT_i", [DA, TL], BF16); KbT_i = nc.dram_tensor("KbT_i", [DA, TL], BF16)
    Va_i = nc.dram_tensor("Va_i", [TL, DA], BF16); Vb_i = nc.dram_tensor("Vb_i", [TL, DA], BF16)
    Ik_i = nc.dram_tensor("Ik_i", [64, TL], BF16); Fl_i = nc.dram_tensor("Fl_i", [TL, H], F32)
    KaT_g = nc.dram_tensor("KaT_g", [4 * DA, TL], BF16); KbT_g = nc.dram_tensor("KbT_g", [4 * DA, TL], BF16)
    Va_g = nc.dram_tensor("Va_g", [4 * TL, DA], BF16); Vb_g = nc.dram_tensor("Vb_g", [4 * TL, DA], BF16)
    Ik_g = nc.dram_tensor("Ik_g", [4 * 64, TL], BF16); Fl_g = nc.dram_tensor("Fl_g", [4 * TL, H], F32)
    dbg = {}
    if debug:
        dbg["ya"] = nc.dram_tensor("dbg_ya", [NG, 128, H, 512], BF16, kind="ExternalOutput")
        dbg["yb"] = nc.dram_tensor("dbg_yb", [NG, 128, H, 512], BF16, kind="ExternalOutput")
        dbg["mask"] = nc.dram_tensor("dbg_mask", [NG, 128, NTG, 512], BF16, kind="ExternalOutput")
        dbg["cpos"] = nc.dram_tensor("dbg_cpos", [128, NTA, H], F32, kind="ExternalOutput")
        dbg["kag"] = nc.dram_tensor("dbg_kag", [4 * DA, TL], BF16, kind="ExternalOutput")
    dram_b = {}

    def DB(name):
        if name not in dram_b:
            dram_b[name] = Buf(name)
        return dram_b[name]

    st = ExitStack()
    SB_BYTES = 207 * 1024
    big = st.enter_context(nc.sbuf_tensor("big", [128, SB_BYTES], U8))
    mem = Mem(big, SB_BYTES)
    banks = []
    for i in range(8):
        t = st.enter_context(nc.psum_tensor("ps%d" % i, [128, 512], F32))
        banks.append(Tl(t[:, :], "ps%d" % i))
    bank_rr = [0]

    def nbank(lo=0, hi=8):
        i = lo + bank_rr[0] % (hi - lo)
        bank_rr[0] += 1
        return banks[i]

    rr = {"ev": 0, "cast": 0}

    def dma(outap, inap, reads, writes, key, q="sp"):
        P.op(q, lambda e: e.dma_start(out=outap, in_=inap), reads, writes, dma_key=key)

    def mm(outap, lhsT, rhs, start, stop, reads, writes):
        P.op("pe", lambda e: e.matmul(outap, lhsT, rhs, start=start, stop=stop), reads, writes)

    def tr(outap, inap, ident, reads, writes):
        P.op("pe", lambda e: e.transpose(outap, inap, ident), reads, writes)

    def act(outap, inap, func, reads, writes, bias=None, scale=None, accum=None):
        kw = {}
        if bias is not None:
            kw["bias"] = bias
        if scale is not None:
            kw["scale"] = scale
        if accum is not None:
            kw["accum_out"] = accum
        P.op("act", lambda e: e.activation(outap, inap, func, **kw), reads, writes)

    def ts(eng, outap, inap, s1, s2, op0, op1, reads, writes):
        if op1 is None:
            P.op(eng, lambda e: e.tensor_scalar(outap, inap, s1, None, op0), reads, writes)
        else:
            P.op(eng, lambda e: e.tensor_scalar(outap, inap, s1, s2, op0, op1), reads, writes)

    def tt(eng, outap, a, b, op, reads, writes):
        P.op(eng, lambda e: e.tensor_tensor(outap, a, b, op), reads, writes)

    def stt(eng, outap, in0, scalar, in1, op0, op1, reads, writes):
        P.op(eng, lambda e: e.scalar_tensor_tensor(outap, in0, scalar, in1, op0, op1), reads, writes)

    def cp(eng, outap, inap, reads, writes):
        if eng == "act":
            P.op("act", lambda e: e.activation(outap, inap, AF.Copy), reads, writes)
        else:
            P.op(eng, lambda e: e.tensor_copy(outap, inap), reads, writes)

    def evac_eng():
        rr["ev"] += 1
        return ("act", "dve")[rr["ev"] % 2]

    ident = bf(mem, 128, "ident")
    ones_bf = bf(mem, 128, "ones_bf")
    ones_f = f32(mem, 128, "ones_f")
    tri_f = f32(mem, 128, "tri_f")
    gmix_sb = f32(mem, DC, "gmix"); gmlp_sb = f32(mem, DC, "gmlp")
    ss = f32(mem, NT, "ss"); rstd = f32(mem, NT, "rstd")
    wfin = f32(mem, NT * HI, "wfin"); wabs = f32(mem, NT * HI, "wabs"); wsgn = f32(mem, NT * HI, "wsgn")
    sel_sb = f32(mem, 4, "sel")
    erb15 = f32(mem, H, "erb15")
    bf_bc = f32(mem, H, "bf_bc")
    negv = f32(mem, 512, "negv")
    valid = bf(mem, 512, "valid")
    cm = bf(mem, 512, "cm")
    tmp512 = f32(mem, 512, "tmp512")
    Cpos = f32(mem, NTA * H, "Cpos")
    Cref = f32(mem, NT * H, "Cref")
    Cpos3 = Cpos.v.rearrange("p (n h) -> p n h", h=H)
    Cref3 = Cref.v.rearrange("p (n h) -> p n h", h=H)
    wabs3 = wabs.v.rearrange("p (n h) -> p n h", h=HI)
    wsgn3 = wsgn.v.rearrange("p (n h) -> p n h", h=HI)
    wfin3 = wfin.v.rearrange("p (n h) -> p n h", h=HI)

    P.op("pool", lambda e: e.memset(ones_bf.v, 1.0), (), (ones_bf.b,))
    P.op("pool", lambda e: e.memset(ones_f.v, 1.0), (), (ones_f.b,))
    P.op("pool", lambda e: e.memset(tri_f.v, 1.0), (), (tri_f.b,))
    P.op("pool", lambda e: e.affine_select(tri_f.v, tri_f.v, [[1, 128]], ALU.is_ge, 0.0, base=0,
                                           channel_multiplier=-1), (tri_f.b,), (tri_f.b,))
    P.op("pool", lambda e: e.memset(tmp512.v[:, 0:128], 1.0), (), (tmp512.b,))
    P.op("pool", lambda e: e.affine_select(tmp512.v[:, 0:128], tmp512.v[:, 0:128], [[1, 128]], ALU.is_equal, 0.0,
                                           base=0, channel_multiplier=-1), (tmp512.b,), (tmp512.b,))
    cp("dve", ident.v, tmp512.v[:, 0:128], (tmp512.b,), (ident.b,))
    P.op("dve", lambda e: e.memset(ss.v, 0.0), (), (ss.b,))
    dma(gmix_sb.v, gmixT, (), (gmix_sb.b,), "c0")
    dma(gmlp_sb.v, gmlpT, (), (gmlp_sb.b,), "c1")
    dma(sel_sb.v, sel_in.partition_broadcast(128), (), (sel_sb.b,), "c2")
    dma(erb15.v, rb15.partition_broadcast(128), (), (erb15.b,), "c3")
    dma(bf_bc.v, bfv.partition_broadcast(128), (), (bf_bc.b,), "c4")
    dma(negv.v.rearrange("p (r s) -> p r s", r=4), negv_in, (), (negv.b,), "c5")
    dma(tmp512.v.rearrange("p (r s) -> p r s", r=4), valid_in, (tmp512.b,), (tmp512.b,), "c6")
    cp("dve", valid.v, tmp512.v, (tmp512.b,), (valid.b,))
    dma(tmp512.v.rearrange("p (r s) -> p r s", r=4), cm_in, (tmp512.b,), (tmp512.b,), "c6")
    cp("dve", cm.v, tmp512.v, (tmp512.b,), (cm.b,))
    act(erb15.v, erb15.v, AF.Exp, (erb15.b,), (erb15.b,))
    base_mark = mem.mark()

    class Precast:
        def __init__(self, KH, NSL, pfx):
            self.KH, self.NSL, self.pfx = KH, NSL, pfx
            self.sf = [f32(mem, KH * 512, pfx + "f%d" % i) for i in range(NSL)]
            self.sb = [bf(mem, KH * 512, pfx + "b%d" % i) for i in range(NSL)]
            self.items = []
            self.n = 0
            self.look = NSL - 1
            self.store_q = "sp"

        def add(self, Wsrc, Wdst, nrows, c0, ncols, blk, gsb):
            KC = nrows // 128
            for k0 in range(0, KC, self.KH):
                self.items.append((Wsrc, Wdst, c0, ncols, blk, gsb, k0, min(self.KH, KC - k0)))

        def load(self, n):
            Wsrc, Wdst, c0, ncols, blk, gsb, k0, kn = self.items[n]
            i = n % self.NSL
            sfv = self.sf[i].v.rearrange("p (k c) -> p k c", c=512)
            src = Wsrc.rearrange("(kc p) c -> p kc c", p=128)
            dma(sfv[:, 0:kn, 0:ncols], src[:, k0:k0 + kn, c0:c0 + ncols], (), (self.sf[i].b,),
                "%sl%d" % (self.pfx, i))

        def cast_store(self, n):
            Wsrc, Wdst, c0, ncols, blk, gsb, k0, kn = self.items[n]
            i = n % self.NSL
            sf, sb_ = self.sf[i], self.sb[i]
            sfv = sf.v.rearrange("p (k c) -> p k c", c=512)
            sbv = sb_.v.rearrange("p (k c) -> p k c", c=512)
            if gsb is None:
                for (ka, kb_) in ((0, kn // 2), (kn // 2, kn)):
                    if kb_ > ka:
                        rr["cast"] += 1
                        eng = ("act", "dve")[rr["cast"] % 2]
                        cp(eng, sbv[:, ka:kb_, 0:ncols], sfv[:, ka:kb_, 0:ncols], (sf.b,), (sb_.b,))
            else:
                for k in range(kn):
                    rr["cast"] += 1
                    eng = ("act", "dve")[rr["cast"] % 2]
                    sc = gsb.v[:, k0 + k:k0 + k + 1]
                    if eng == "act":
                        act(sbv[:, k, 0:ncols], sfv[:, k, 0:ncols], AF.Copy, (sf.b, gsb.b), (sb_.b,), scale=sc)
                    else:
                        ts(eng, sbv[:, k, 0:ncols], sfv[:, k, 0:ncols], sc, None, ALU.mult, None,
                           (sf.b, gsb.b), (sb_.b,))
            dma(Wdst.ap()[blk, :, k0:k0 + kn, 0:ncols], sbv[:, 0:kn, 0:ncols], (sb_.b,), (DB(Wdst.name),),
                "%ss%d" % (self.pfx, i), q=self.store_q)

        def step(self):
            n = self.n
            if n >= len(self.items):
                return False
            if n == 0:
                for q in range(min(self.look, len(self.items))):
                    self.load(q)
            if n + self.look < len(self.items):
                self.load(n + self.look)
            self.cast_store(n)
            self.n += 1
            return True

    pc1 = Precast(16 if DC >= 16 else DC, 3, "p0")
    for bi, (kind, idx, c0, ncols) in enumerate(blocks):
        pc1.add(w_in, Wt_in, D, c0, ncols, bi, gmix_sb)
    while pc1.step():
        pass
    P.barrier()
    mem.release(base_mark)
    pc2 = Precast(8 if DC >= 8 else DC, 2, "p2")
    pc2.store_q = "pool"
    for b in range(D // 512):
        pc2.add(w_oa, Wt_oa, DA, b * 512, 512, b, None)
        pc2.add(w_ob, Wt_ob, DA, b * 512, 512, b, None)
        pc2.add(w_o, Wt_o, D, b * 512, 512, b, None)
    for b in range(DFF // 512):
        pc2.add(w_up, Wt_up, D, b * 512, 512, b, gmlp_sb)
    for b in range(D // 512):
        pc2.add(w_dn, Wt_dn, DFF, b * 512, 512, b, None)
    p2_per_iter = 1
    a_mark = mem.mark()

    def rstd_from_ss(i):
        ts("dve", rstd.v[:, i:i + 1], ss.v[:, i:i + 1], 1.0 / D, EPS, ALU.mult, ALU.add, (ss.b,), (rstd.b,))
        act(rstd.v[:, i:i + 1], rstd.v[:, i:i + 1], AF.Sqrt, (rstd.b,), (rstd.b,))
        P.op("dve", (lambda i_: (lambda e: e.reciprocal(rstd.v[:, i_:i_ + 1], rstd.v[:, i_:i_ + 1])))(i),
             (rstd.b,), (rstd.b,))

    def norm_transpose(src_tile, src_view, m_idx, xn_t, dstT, dst_b, col0, junk):
        P.op("dve", (lambda mi: (lambda e: e.memset(ss.v[:, mi:mi + 1], 0.0)))(m_idx), (), (ss.b,))
        act(junk.v, src_view, AF.Square, (src_tile.b,), (junk.b, ss.b), accum=ss.v[:, m_idx:m_idx + 1])
        rstd_from_ss(m_idx)
        ts("dve", xn_t.v, src_view, rstd.v[:, m_idx:m_idx + 1], None, ALU.mult, None,
           (src_tile.b, rstd.b), (xn_t.b,))
        for c4 in range(0, DC, 4):
            n4 = min(4, DC - c4)
            bk = nbank()
            bv = bk.v.bitcast(BF16)
            for q in range(n4):
                tr(bv[:, q * 128:(q + 1) * 128], xn_t.v[:, (c4 + q) * 128:(c4 + q + 1) * 128], ident.v,
                   (xn_t.b, ident.b), (bk.b,))
            cp(evac_eng(), dstT[:, c4:c4 + n4, col0:col0 + 128],
               bv[:, 0:n4 * 128].rearrange("p (q t) -> p q t", q=n4), (bk.b,), (dst_b,))

    xnT = bf(mem, DC * 512, "xnT")
    xnT3 = xnT.v.rearrange("p (c t) -> p c t", t=512)
    wsl = [bf(mem, DC * 512, "w%d" % i) for i in range(2)]
    xts = [f32(mem, D, "xt0")] * 2
    xns = [bf(mem, D, "xn0")] * 2
    junk = xns[0]
    stage = [bf(mem, 4 * 512, "stage%d" % i) for i in range(2)]
    stage_f = [f32(mem, 4 * 32, "stagef%d" % i) for i in range(2)]
    wcnt = [0]
    scnt = [0]

    def load_w(Wt, blk, kc0, kcn, ncols=512):
        i = wcnt[0] % 2
        wcnt[0] += 1
        w = wsl[i]
        wv = w.v.rearrange("p (k c) -> p k c", c=512)
        dma(wv[:, 0:kcn, 0:ncols], Wt.ap()[blk, :, kc0:kc0 + kcn, 0:ncols], (DB(Wt.name),), (w.b,), "w%d" % i)
        return w, wv

    def fm_group(wv, w, kcn, c_lo, c_n, rhs_fn, rhs_reads, ntok):
        bk = nbank()
        for kc in range(kcn):
            mm(bk.v[0:c_n, 0:ntok], wv[:, kc, c_lo:c_lo + c_n], rhs_fn(kc), kc == 0, kc == kcn - 1,
               (w.b,) + rhs_reads, (bk.b,))
        return bk

    for G in range(NG):
        for t4 in range(4):
            m = 4 * G + t4
            xt, xn = xts[m % 2], xns[m % 2]
            dma(xt.v, x_in[m * 128:(m + 1) * 128, :], (), (xt.b,), "xt%d" % (m % 2))
            norm_transpose(xt, xt.v, m, xn, xnT3, xnT.b, t4 * 128, junk)
        tsl = slice(G * 512, (G + 1) * 512)
        for bi, (kind, idx, c0, ncols) in enumerate(blocks):
            w, wv = load_w(Wt_in, bi, 0, DC, ncols)
            if kind in ("qa", "ka", "qb", "kb", "iq", "ga", "gb"):
                sg = stage[scnt[0] % 2]
                sgi = scnt[0] % 2
                scnt[0] += 1
                sgv = sg.v.rearrange("p (q t) -> p q t", t=512)
                nq = ncols // 128
                for q in range(nq):
                    bk = fm_group(wv, w, DC, q * 128, 128, lambda kc: xnT3[:, kc, :], (xnT.b,), 512)
                    if kind in ("ga", "gb"):
                        act(sgv[:, q, :], bk.v, AF.Sigmoid, (bk.b,), (sg.b,))
                    else:
                        cp(evac_eng(), sgv[:, q, :], bk.v, (bk.b,), (sg.b,))
                dst = {"qa": QaT, "qb": QbT, "ka": KaT_i, "kb": KbT_i, "iq": IqT, "ga": SgA, "gb": SgB}[kind]
                dma(dst.ap()[idx * 512:idx * 512 + ncols, tsl].rearrange("(q p) t -> p q t", p=128),
                    sgv[:, 0:nq, :], (sg.b,), (DB(dst.name),), "st%d" % sgi, q="pool")
            elif kind in ("va", "vb"):
                sg = stage[scnt[0] % 2]
                sgi = scnt[0] % 2
                scnt[0] += 1
                sgv = sg.v.rearrange("p (q t) -> p q t", t=512)
                for t4 in range(4):
                    bk = nbank()
                    for kc in range(DC):
                        mm(bk.v[:, 0:ncols], xnT3[:, kc, t4 * 128:(t4 + 1) * 128], wv[:, kc, 0:ncols],
                           kc == 0, kc == DC - 1, (w.b, xnT.b), (bk.b,))
                    cp(evac_eng(), sgv[:, t4, 0:ncols], bk.v[:, 0:ncols], (bk.b,), (sg.b,))
                dst = Va_i if kind == "va" else Vb_i
                dma(dst.ap()[G * 512:(G + 1) * 512, idx * 512:idx * 512 + ncols].rearrange("(q p) e -> p q e", p=128),
                    sgv[:, :, 0:ncols], (sg.b,), (DB(dst.name),), "st%d" % sgi, q="pool")
            elif kind == "misc":
                sg = stage[scnt[0] % 2]
                sgi = scnt[0] % 2
                scnt[0] += 1
                bk = fm_group(wv, w, DC, 0, 64, lambda kc: xnT3[:, kc, :], (xnT.b,), 512)
                cp(evac_eng(), sg.v[0:64, 0:512], bk.v[0:64, :], (bk.b,), (sg.b,))
                dma(Ik_i.ap()[:, tsl], sg.v[0:64, 0:512], (sg.b,), (DB("Ik_i"),), "st%d" % sgi, q="pool")
                for t4 in range(4):
                    m = 4 * G + t4
                    bk = nbank()
                    for kc in range(DC):
                        mm(bk.v[:, 0:HI], xnT3[:, kc, t4 * 128:(t4 + 1) * 128], wv[:, kc, 64:64 + HI],
                           kc == 0, kc == DC - 1, (w.b, xnT.b), (bk.b,))
                    ts("dve", wfin3[:, m, :], bk.v[:, 0:HI], (HI ** -0.5) * (64 ** -0.5), None, ALU.mult, None,
                       (bk.b,), (wfin.b,))
            elif kind == "fl":
                sf = stage_f[scnt[0] % 2]
                sfi = scnt[0] % 2
                scnt[0] += 1
                sfv = sf.v.rearrange("p (q h) -> p q h", h=32)
                for t4 in range(4):
                    bk = nbank()
                    for kc in range(DC):
                        mm(bk.v[:, 0:H], xnT3[:, kc, t4 * 128:(t4 + 1) * 128], wv[:, kc, 0:H],
                           kc == 0, kc == DC - 1, (w.b, xnT.b), (bk.b,))
                    cp("dve", sfv[:, t4, 0:H], bk.v[:, 0:H], (bk.b,), (sf.b,))
                dma(Fl_i.ap()[G * 512:(G + 1) * 512, :].rearrange("(q p) h -> p q h", p=128), sfv[:, :, 0:H],
                    (sf.b,), (DB("Fl_i"),), "sf%d" % sfi, q="pool")
            for _ in range(p2_per_iter):
                pc2.step()
    ts("dve", wabs.v, wfin.v, -1.0, None, ALU.mult, None, (wfin.b,), (wabs.b,))
    tt("dve", wabs.v, wabs.v, wfin.v, ALU.max, (wabs.b, wfin.b), (wabs.b,))
    ts("dve", wsgn.v, wfin.v, 0.0, 2.0, ALU.is_ge, ALU.mult, (wfin.b,), (wsgn.b,))
    ts("dve", wsgn.v, wsgn.v, -1.0, None, ALU.add, None, (wsgn.b,), (wsgn.b,))

    groups = [[0, 1, 2, 3], [4, 5, 6, 7]]

    def allgather(src_ap, dst_ap, sname, dname):
        P.op("pool", lambda e: e.collective_compute("AllGather", ALU.bypass, replica_groups=groups,
                                                    ins=[src_ap], outs=[dst_ap]),
             (DB(sname),), (DB(dname),), dma_key="cc", inc_amt=1)

    allgather(Ik_i.ap(), Ik_g.ap(), "Ik_i", "Ik_g")
    allgather(Fl_i.ap(), Fl_g.ap(), "Fl_i", "Fl_g")
    for (src, dst, nslab) in ((KaT_i, KaT_g, H), (Va_i, Va_g, NT), (KbT_i, KbT_g, H), (Vb_i, Vb_g, NT)):
        for i in range(nslab):
            allgather(src.ap()[i * 128:(i + 1) * 128, :], dst.ap()[i * 512:(i + 1) * 512, :], src.name, dst.name)

    pc2.store_q = "sp"
    while pc2.step():
        pass
    P.barrier(skip_keys=("cc",))
    mem.release(base_mark)

    L = f32(mem, NTA * H, "L")
    L3 = L.v.rearrange("p (n h) -> p n h", h=H)
    totbc = f32(mem, NTA * H, "totbc")
    tot3 = totbc.v.rearrange("p (n h) -> p n h", h=H)
    carry = f32(mem, NTA * H, "carry")
    car3 = carry.v.rearrange("p (n h) -> p n h", h=H)
    for r in range(4):
        dma(L3[:, r * NT:(r + 1) * NT, :],
            Fl_g.ap()[r * TL:(r + 1) * TL, :].rearrange("(n p) h -> p n h", p=128), (DB("Fl_g"),), (L.b,), "c0")
    for n in range(NTA):
        tt("dve", L3[:, n, :], L3[:, n, :], bf_bc.v, ALU.add, (L.b, bf_bc.b), (L.b,))
    act(L.v, L.v, AF.Exp, (L.b,), (L.b,), scale=-1.0)
    act(L.v, L.v, AF.Ln, (L.b,), (L.b,), bias=1.0)
    NF = NTA * H
    for c0 in range(0, NF, 512):
        cn = min(512, NF - c0)
        bk = nbank()
        mm(bk.v[:, 0:cn], tri_f.v, L.v[:, c0:c0 + cn], True, True, (tri_f.b, L.b), (bk.b,))
        cp("dve", Cpos.v[:, c0:c0 + cn], bk.v[:, 0:cn], (bk.b,), (Cpos.b,))
        bk = nbank()
        mm(bk.v[:, 0:cn], ones_f.v, L.v[:, c0:c0 + cn], True, True, (ones_f.b, L.b), (bk.b,))
        cp("dve", totbc.v[:, c0:c0 + cn], bk.v[:, 0:cn], (bk.b,), (totbc.b,))

    def nidx(g):
        return (g % 4) * NT + g // 4

    P.op("dve", lambda e: e.memset(carry.v, 0.0), (), (carry.b,))
    for g in range(1, NTG):
        tt("dve", car3[:, nidx(g), :], car3[:, nidx(g - 1), :], tot3[:, nidx(g - 1), :], ALU.add,
           (carry.b, totbc.b), (carry.b,))
    tt("dve", Cpos.v, Cpos.v, carry.v, ALU.add, (Cpos.b, carry.b), (Cpos.b,))
    stt("dve", totbc.v, totbc.v, 0.5, carry.v, ALU.mult, ALU.add, (totbc.b, carry.b), (totbc.b,))
    for r in range(4):
        src = tot3[:, r * NT:(r + 1) * NT, :]
        if r == 0:
            ts("dve", Cref3, src, sel_sb.v[:, 0:1], None, ALU.mult, None, (totbc.b, sel_sb.b), (Cref.b,))
        else:
            stt("dve", Cref3, src, sel_sb.v[:, r:r + 1], Cref3, ALU.mult, ALU.add,
                (totbc.b, sel_sb.b, Cref.b), (Cref.b,))
    if debug:
        dma(dbg["cpos"].ap(), Cpos3, (Cpos.b,), (DB("dbgc"),), "dbgs")
    P.barrier(skip_keys=("cc",))
    mem.release(base_mark)

    maskT = Tl(mem.alloc_top(NTG * 512 * 2, BF16), "maskT")
    maskT3 = maskT.v.rearrange("p (k t) -> p k t", t=512)
    grp_mark = mem.mark()
    SCALE = 128 ** -0.5

    for G in range(NG):
        NKT = 4 * (4 * G + 4)
        ikT = bf(mem, 4 * TL, "ikT")
        ikT3 = ikT.v.rearrange("p (r t) -> p r t", r=4)
        dma(ikT3[0:64], Ik_g.ap().rearrange("(r d) t -> d r t", d=64), (DB("Ik_g"),), (ikT.b,), "c0")
        dma(ikT3[64:128], Ik_g.ap().rearrange("(r d) t -> d r t", d=64), (DB("Ik_g"),), (ikT.b,), "c0")
        scores = [f32(mem, NTG * 128, "score%d" % i) for i in range(2)]
        m01s = [bf(mem, 512, "m01_%d" % i) for i in range(2)]
        iqs = [bf(mem, HI * 128, "iq%d" % i) for i in range(2)]
        rl = [bf(mem, 512, "rl%d" % i) for i in range(6)]
        dg = bf(mem, HI * 128, "dg")
        dg3 = dg.v.rearrange("p (h t) -> p h t", t=128)
        mx8s = [f32(mem, 8, "mx8_%d" % i) for i in range(2)]
        rcnt_box = [0]
        def score_phase(t4):
            m = 4 * G + t4
            score = scores[m % 2]
            iq = iqs[m % 2]
            iq3 = iq.v[:, 0:(HI // 2) * 128].rearrange("p (c t) -> p c t", t=128)
            dma(iq3, IqT.ap().rearrange("(c p) t -> p c t", p=128)[:, :, m * 128:(m + 1) * 128],
                (DB("IqT"),), (iq.b,), "iq%d" % (m % 2))
            for h in range(HI):
                ts("pool", dg3[:, h, :], ident.v, wsgn3[:, m, h:h + 1], None, ALU.mult, None,
                   (ident.b, wsgn.b), (dg.b,))
            seq = [(kb, c) for kb in range(m + 1) for c in range(HI // 2)]
            rls = [None] * len(seq)

            def s1(i):
                kb, c = seq[i]
                pair = []
                bks = [nbank(0, 6), nbank(0, 6)]
                for u in range(2):
                    p0 = 64 * u
                    mm(bks[u].v.rearrange("p (r s) -> p r s", r=4), iq3[p0:p0 + 64, c, :],
                       ikT3[p0:p0 + 64, :, kb * 128:(kb + 1) * 128], True, True, (iq.b, ikT.b), (bks[u].b,))
                for u in range(2):
                    h = 2 * c + u
                    r_ = rl[rcnt_box[0] % 6]
                    rcnt_box[0] += 1
                    pair.append(r_)
                    act(r_.v, bks[u].v, AF.Relu, (bks[u].b, wabs.b), (r_.b,), scale=wabs3[:, m, h:h + 1])
                rls[i] = pair

            def s2(i):
                kb, c = seq[i]
                accb = banks[6 + kb % 2]
                for u in range(2):
                    h = 2 * c + u
                    mm(accb.v, dg3[:, h, :], rls[i][u].v, h == 0, h == HI - 1, (dg.b, rls[i][u].b), (accb.b,))
                if c == HI // 2 - 1:
                    sl = slice(kb * 512, (kb + 1) * 512)
                    cp("act", score.v[:, sl], accb.v, (accb.b,), (score.b,))

            LA1 = 2
            for i in range(len(seq) + LA1):
                if i < len(seq):
                    s1(i)
                if i >= LA1:
                    s2(i - LA1)

        def topk_phase(t4):
            m = 4 * G + t4
            score = scores[m % 2]
            mx8 = mx8s[m % 2]
            nk = 512 * (m + 1)
            dsl = slice(m * 512, (m + 1) * 512)
            tt("dve", score.v[:, dsl], score.v[:, dsl], negv.v, ALU.add, (score.b, negv.b), (score.b,))
            for it in range(cfg.TOPK // 8):
                P.op("dve", (lambda nk_, sc_, mx_: (lambda e: e.max(out=mx_.v, in_=sc_.v[:, 0:nk_])))(nk, score, mx8),
                     (score.b,), (mx8.b,))
                P.op("dve", (lambda nk_, sc_, mx_: (lambda e: e.match_replace(
                    out=sc_.v[:, 0:nk_], in_to_replace=mx_.v, in_values=sc_.v[:, 0:nk_], imm_value=REPL)))(nk, score, mx8),
                     (score.b, mx8.b), (score.b,))
            for k4 in range(0, 4 * (m + 1), 4):
                m01 = m01s[(k4 // 4) % 2]
                P.op("dve", (lambda mo, k4_, sc_: (lambda e: e.tensor_single_scalar(
                    mo.v, sc_.v[:, k4_ * 128:(k4_ + 4) * 128], REPL, ALU.is_equal)))(m01, k4, score),
                    (score.b,), (m01.b,))
                if k4 == 4 * m:
                    tt("dve", m01.v, m01.v, valid.v, ALU.mult, (m01.b, valid.b), (m01.b,))
                bk = nbank(0, 6)
                bv = bk.v.bitcast(BF16)
                for q in range(4):
                    tr(bv[:, q * 128:(q + 1) * 128], m01.v[:, q * 128:(q + 1) * 128], ident.v,
                       (m01.b, ident.b), (bk.b,))
                cp("act", maskT3[:, k4:k4 + 4, t4 * 128:(t4 + 1) * 128],
                   bv[:, 0:512].rearrange("p (q t) -> p q t", q=4), (bk.b,), (maskT.b,))

        score_phase(0)
        for t4 in range(4):
            if t4 + 1 < 4:
                score_phase(t4 + 1)
            topk_phase(t4)
        if debug:
            dma(dbg["mask"].ap()[G, :, 0:NKT, :], maskT3[:, 0:NKT, :], (maskT.b,), (DB("dbgm"),), "dbgs")
        P.barrier(skip_keys=("cc",) if G == 0 else ())
        mem.release(grp_mark)
        yaT = bf(mem, H * 512, "yaT"); ybT = bf(mem, H * 512, "ybT")
        yaT3 = yaT.v.rearrange("p (h t) -> p h t", t=512)
        ybT3 = ybT.v.rearrange("p (h t) -> p h t", t=512)
        yab_mark = mem.mark()

        qs = [bf(mem, 512, "q%d" % i) for i in range(2)]
        kts = [bf(mem, 4 * TL, "kt%d" % i) for i in range(2)]
        vs = [bf(mem, 4 * NT * 128, "v%d" % i) for i in range(2)]
        ebf = [f32(mem, 12 * 128, "ebf0")] * 2
        ebs = [bf(mem, 12 * 128, "eb%d" % i) for i in range(2)]
        pts = [bf(mem, 512, "pt%d" % i) for i in range(4)]
        Bh = [f32(mem, 4 * NTA, "Bh%d" % i) for i in range(2)]
        rden = f32(mem, 512, "rden")
        accPs = [f32(mem, 512, "accP%d" % i) for i in range(2)]
        pcnt_box = [0]
        hcnt = 0
        MK = 4 * G + 4
        for branch in ("A", "B"):
            QT = QaT if branch == "A" else QbT
            KTg = KaT_g if branch == "A" else KbT_g
            Vg = Va_g if branch == "A" else Vb_g
            yT3, yT = (yaT3, yaT) if branch == "A" else (ybT3, ybT)
            for h in range(H):
                i2 = hcnt % 2
                hcnt += 1
                q_, kt, v_ = qs[i2], kts[i2], vs[i2]
                kt3 = kt.v.rearrange("p (r t) -> p r t", r=4)
                v4 = v_.v.rearrange("p (r m d) -> p r m d", r=4, d=128)
                dma(q_.v, QT.ap()[h * 128:(h + 1) * 128, G * 512:(G + 1) * 512], (DB(QT.name),), (q_.b,), "q%d" % i2)
                dma(kt3[:, :, 0:MK * 128],
                    KTg.ap().rearrange("(h r d) t -> d h r t", r=4, h=H)[:, h, :, 0:MK * 128],
                    (DB(KTg.name),), (kt.b,), "kt%d" % i2)
                for r in range(4):
                    dma(v4[:, r, 0:MK, :],
                        Vg.ap().rearrange("(m r p) e -> p m r e", r=4, p=128)[:, 0:MK, r, h * 128:(h + 1) * 128],
                        (DB(Vg.name),), (v_.b,), "v%d" % i2)
                if branch == "A":
                    e_f, e_b = ebf[i2], ebs[i2]
                    dma(e_f.v.rearrange("p (a r t) -> p a r t", a=3, r=4), EB[h], (), (e_f.b,), "ebl")
                    act(e_b.v, e_f.v, AF.Exp, (e_f.b,), (e_b.b,))
                    e4 = e_b.v.rearrange("p (a r t) -> p a r t", a=3, r=4)
                else:
                    bh = Bh[i2]
                    bh3 = bh.v.rearrange("p (t n) -> p t n", t=4)
                    for t4 in range(4):
                        m = 4 * G + t4
                        ts("dve", bh3[:, t4, :], Cpos3[:, :, h], Cref3[:, m, h:h + 1], 80.0, ALU.subtract, ALU.min,
                           (Cpos.b, Cref.b), (bh.b,))
                bO, bD = (banks[4], banks[5]) if i2 == 0 else (banks[6], banks[7])
                tiles = [(mk, r) for mk in range(MK) for r in range(4)]
                nTl = len(tiles)
                ptl = [None] * nTl

                def stage1(i):
                    mk, r = tiles[i]
                    a0 = max(0, mk - 4 * G)
                    c0 = a0 * 128
                    kt_idx = 4 * mk + r
                    bk = nbank(0, 4)
                    mm(bk.v[:, c0:512], kt3[:, r, mk * 128:(mk + 1) * 128], q_.v[:, c0:512], True, True,
                       (kt.b, q_.b), (bk.b,))
                    pt = pts[pcnt_box[0] % 4]
                    pcnt_box[0] += 1
                    ptl[i] = pt
                    if branch == "A":
                        act(pt.v[:, c0:512], bk.v[:, c0:512], AF.Exp, (bk.b,), (pt.b,), scale=SCALE)
                        tf = max(a0, mk + 3 - 4 * G)
                        if tf < 4:
                            cs = slice(tf * 128, 512)
                            stt("dve", pt.v[:, cs], pt.v[:, cs], erb15.v[:, h:h + 1], maskT3[:, kt_idx, cs],
                                ALU.mult, ALU.mult, (pt.b, erb15.b, maskT.b), (pt.b,))
                        for t4 in range(a0, min(tf, 4)):
                            dm_ = 4 * G + t4 - mk
                            cs = slice(t4 * 128, (t4 + 1) * 128)
                            tt("dve", pt.v[:, cs], pt.v[:, cs], maskT3[:, kt_idx, cs], ALU.mult,
                               (pt.b, maskT.b), (pt.b,))
                            tt("pool", pt.v[:, cs], pt.v[:, cs], e4[:, dm_, r, :], ALU.mult,
                               (pt.b, e_b.b), (pt.b,))
                    else:
                        n_ = r * NT + mk
                        for t4 in range(a0, 4):
                            cs = slice(t4 * 128, (t4 + 1) * 128)
                            act(pt.v[:, cs], bk.v[:, cs], AF.Exp, (bk.b, bh.b), (pt.b,), scale=SCALE,
                                bias=bh3[:, t4, n_:n_ + 1])
                            if 4 * G + t4 == mk:
                                tt("pool", pt.v[:, cs], pt.v[:, cs], cm.v[:, r * 128:(r + 1) * 128], ALU.mult,
                                   (pt.b, cm.b), (pt.b,))

                def stage2(i):
                    mk, r = tiles[i]
                    c0 = max(0, mk - 4 * G) * 128
                    pt = ptl[i]
                    mm(bO.v[:, c0:512], v4[:, r, mk, :], pt.v[:, c0:512], i == 0, i == nTl - 1, (v_.b, pt.b), (bO.b,))
                    if branch == "A":
                        mm(bD.v[:, c0:512], ones_bf.v, pt.v[:, c0:512], i == 0, i == nTl - 1,
                           (ones_bf.b, pt.b), (bD.b,))
                    else:
                        accP = accPs[i2]
                        if i == 0:
                            cp("dve", accP.v, pt.v, (pt.b,), (accP.b,))
                        else:
                            tt("dve", accP.v[:, c0:512], accP.v[:, c0:512], pt.v[:, c0:512], ALU.add,
                               (accP.b, pt.b), (accP.b,))
                        if i == nTl - 1:
                            mm(bD.v, ones_f.v, accP.v, True, True, (ones_f.b, accP.b), (bD.b,))

                LA = 2
                for i in range(nTl + LA):
                    if i < nTl:
                        stage1(i)
                    if i >= LA:
                        stage2(i - LA)
                P.op("dve", (lambda rd, bd: (lambda e: e.reciprocal(rd.v, bd.v)))(rden, bD), (bD.b,), (rden.b,))
                tt("dve", yT3[:, h, :], bO.v, rden.v, ALU.mult, (bO.b, rden.b), (yT.b,))
        if debug:
            dma(dbg["ya"].ap()[G], yaT3, (yaT.b,), (DB("dbgya"),), "dbgs")
            dma(dbg["yb"].ap()[G], ybT3, (ybT.b,), (DB("dbgyb"),), "dbgs")
        P.barrier()
        mem.release(yab_mark)

        TT = 256
        top_size = mem.size
        mem.size = SB_BYTES
        wsl2 = [bf(mem, 32 * 512, "wc%d" % i) for i in range(2)]
        x1 = f32(mem, 2 * D, "x1")
        x13 = x1.v.rearrange("p (t d) -> p t d", t=2)
        mh = bf(mem, DC * TT, "mh")
        mh3 = mh.v.rearrange("p (c t) -> p c t", t=TT)
        FQ = min(32, FC)
        hT = bf(mem, FQ * TT, "hT")
        hT3 = hT.v.rearrange("p (c t) -> p c t", t=TT)
        xn2 = bf(mem, D, "xn2")
        junk2 = xn2
        assert FQ * TT * 2 >= 0
        gf_t = f32(mem, D, "gfin_bc") if FQ * TT * 2 < D * 4 else None
        gfv = gf_t.v if gf_t is not None else hT.v.bitcast(F32)[:, 0:D]
        gfb = gf_t.b if gf_t is not None else hT.b
        sgs = [bf(mem, 2 * TT, "sg%d" % i) for i in range(2)]
        tA = f32(mem, TT, "tA"); tB = f32(mem, TT, "tB")
        rtmp = [f32(mem, TT, "rtmp%d" % i) for i in range(2)]
        wc = [0]

        def load_w2(Wt, blk, kc0, kcn):
            i = wc[0] % 2
            wc[0] += 1
            w = wsl2[i]
            wv = w.v.rearrange("p (k c) -> p k c", c=512)
            dma(wv[:, 0:kcn, :], Wt.ap()[blk, :, kc0:kc0 + kcn, :], (DB(Wt.name),), (w.b,), "wc%d" % i)
            return w, wv

        for sub in range(2):
            tok0 = G * 512 + sub * TT
            ysl = slice(sub * TT, (sub + 1) * TT)
            dma(x13, x_in[tok0:tok0 + TT, :].rearrange("(t p) d -> p t d", p=128), (), (x1.b,), "x1")
            for nb in range(D // 512):
                wa, wav = load_w2(Wt_oa, nb, 0, H)
                wb, wbv = load_w2(Wt_ob, nb, 0, H)
                for q in range(4):
                    dmc = nb * 4 + q
                    sg = sgs[dmc % 2]
                    sgv = sg.v.rearrange("p (a t) -> p a t", a=2)
                    dma(sgv[:, 0, :], SgA.ap()[dmc * 128:(dmc + 1) * 128, tok0:tok0 + TT], (DB("SgA"),), (sg.b,),
                        "sg%d" % (dmc % 2))
                    dma(sgv[:, 1, :], SgB.ap()[dmc * 128:(dmc + 1) * 128, tok0:tok0 + TT], (DB("SgB"),), (sg.b,),
                        "sg%d" % (dmc % 2))
                    bA = fm_group(wav, wa, H, q * 128, 128, lambda kc: yaT3[:, kc, ysl], (yaT.b,), TT)
                    bB = fm_group(wbv, wb, H, q * 128, 128, lambda kc: ybT3[:, kc, ysl], (ybT.b,), TT)
                    tt("dve", tA.v, bA.v[:, 0:TT], sgv[:, 0, :], ALU.mult, (bA.b, sg.b), (tA.b,))
                    tt("dve", tB.v, bB.v[:, 0:TT], sgv[:, 1, :], ALU.mult, (bB.b, sg.b), (tB.b,))
                    tt("pool", mh3[:, dmc, :], tA.v, tB.v, ALU.add, (tA.b, tB.b), (mh.b,))
            for nb in range(D // 512):
                w, wv = load_w2(Wt_o, nb, 0, DC)
                for t2 in range(2):
                    bk = nbank()
                    for kc in range(DC):
                        mm(bk.v, mh3[:, kc, t2 * 128:(t2 + 1) * 128], wv[:, kc, :], kc == 0, kc == DC - 1,
                           (w.b, mh.b), (bk.b,))
                    tt("dve", x13[:, t2, nb * 512:(nb + 1) * 512], x13[:, t2, nb * 512:(nb + 1) * 512], bk.v, ALU.add,
                       (bk.b, x1.b), (x1.b,))
            for t2 in range(2):
                norm_transpose(x1, x13[:, t2, :], 0, xn2, mh3, mh.b, t2 * 128, junk2)
            for fq in range(0, FC, FQ):
                for fb in range(FQ // 4):
                    w, wv = load_w2(Wt_up, (fq // 4) + fb, 0, DC)
                    for q in range(4):
                        bk = fm_group(wv, w, DC, q * 128, 128, lambda kc: mh3[:, kc, :], (mh.b,), TT)
                        r_ = rtmp[(fb * 4 + q) % 2]
                        act(r_.v, bk.v[:, 0:TT], AF.Relu, (bk.b,), (r_.b,))
                        tt("pool", hT3[:, fb * 4 + q, :], r_.v, r_.v, ALU.mult, (r_.b,), (hT.b,))
                for nb in range(D // 512):
                    w, wv = load_w2(Wt_dn, nb, fq, FQ)
                    for t2 in range(2):
                        bk = nbank()
                        for fc in range(FQ):
                            mm(bk.v, hT3[:, fc, t2 * 128:(t2 + 1) * 128], wv[:, fc, :], fc == 0, fc == FQ - 1,
                               (w.b, hT.b), (bk.b,))
                        tt("dve", x13[:, t2, nb * 512:(nb + 1) * 512], x13[:, t2, nb * 512:(nb + 1) * 512], bk.v,
                           ALU.add, (bk.b, x1.b), (x1.b,))
            dma(gfv, gfin.partition_broadcast(128), (), (gfb,), "c1")
            for t2 in range(2):
                P.op("dve", lambda e: e.memset(ss.v[:, 1:2], 0.0), (), (ss.b,))
                act(junk2.v, x13[:, t2, :], AF.Square, (x1.b,), (junk2.b, ss.b), accum=ss.v[:, 1:2])
                rstd_from_ss(1)
                stt("dve", x13[:, t2, :], x13[:, t2, :], rstd.v[:, 1:2], gfv, ALU.mult, ALU.mult,
                    (x1.b, rstd.b, gfb), (x1.b,))
            dma(out[tok0:tok0 + TT, :].rearrange("(t p) d -> p t d", p=128), x13, (x1.b,), (DB("out"),), "x1o",
                q="pool")
        P.barrier()
        mem.release(grp_mark)
        mem.size = top_size

    P.emit(nc, st)
    st.close()
    return nc


def t5_bucket_np(rel, nbk=32, maxd=1024):
    half = nbk // 2
    max_exact = half // 2
    base = np.where(rel > 0, half, 0)
    n = np.abs(rel)
    nf = np.maximum(n, max_exact).astype(np.float32)
    large = max_exact + (np.log(nf / np.float32(max_exact)) / np.float32(math.log(maxd / max_exact))
                         * np.float32(half - max_exact)).astype(np.int32)
    large = np.minimum(large, half - 1)
    return base + np.where(n < max_exact, n, large)


def core_consts(cfg, j, rel_bias):
    H = cfg.H
    p = np.arange(128)
    sel = np.zeros((1, 4), np.float32); sel[0, j] = 1.0
    negv = np.zeros((128, 4, 128), np.float32)
    valid = np.zeros((128, 4, 128), np.float32)
    cm = np.zeros((128, 4, 128), np.float32)
    for r in range(4):
        kpos = r * 128 + p[None, :]
        qpos = j * 128 + p[:, None]
        vis_chunk = kpos < (qpos // 64 + 1) * 64
        valid[:, r, :] = vis_chunk
        negv[:, r, :] = np.where(vis_chunk, 0.0, NEGFILL)
        cm[:, r, :] = ((r * 128 + p[:, None]) <= (j * 128 + p[None, :]))
    eb = np.zeros((H, 128, 3, 4, 128), np.float32)
    for dm in range(3):
        for r in range(4):
            delta = -4 * dm + r - j
            rel = delta * 128 + p[:, None] - p[None, :]
            bk = t5_bucket_np(rel, cfg.NBK, cfg.MAXD)
            eb[:, :, dm, r, :] = np.transpose(rel_bias[bk], (2, 0, 1))
    return sel, negv, valid, cm, eb


def make_in_maps(cfg, inputs):
    x = np.asarray(inputs["x"], np.float32)
    rel_bias = np.asarray(inputs["rel_bias"], np.float32)
    DC = cfg.DC
    shared = {
        "w_in": np.ascontiguousarray(inputs["w_in"][0]),
        "w_out_a": np.ascontiguousarray(inputs["w_out_a"][0]),
        "w_out_b": np.ascontiguousarray(inputs["w_out_b"][0]),
        "w_o": np.ascontiguousarray(inputs["w_o"][0]),
        "w_up": np.ascontiguousarray(inputs["w_up"][0]),
        "w_down": np.ascontiguousarray(inputs["w_down"][0]),
        "b_f": np.ascontiguousarray(inputs["b_f"][0:1]),
        "g_mixT": np.ascontiguousarray(np.asarray(inputs["g_mix"][0]).reshape(DC, 128).T),
        "g_mlpT": np.ascontiguousarray(np.asarray(inputs["g_mlp"][0]).reshape(DC, 128).T),
        "g_final": np.ascontiguousarray(inputs["g_final"]),
        "rb15": np.ascontiguousarray(rel_bias[15:16, :]),
    }
    maps = []
    for c in range(8):
        b, j = c // 4, c % 4
        xt = x[b].reshape(cfg.NT, 4, 128, cfg.D)[:, j]
        sel, negv, valid, cm, eb = core_consts(cfg, j, rel_bias)
        d = dict(shared)
        d.update({"x": np.ascontiguousarray(xt.reshape(cfg.TL, cfg.D)), "EB": eb, "sel": sel, "negv": negv,
                  "valid": valid, "cm": cm})
        maps.append(d)
    return maps


def assemble(cfg, results):
    out = np.zeros((2, cfg.NT, 4, 128, cfg.D), np.float32)
    for c in range(8):
        b, j = c // 4, c % 4
        out[b, :, j] = np.asarray(results[c]["out"]).reshape(cfg.NT, 128, cfg.D)
    return out.reshape(2, cfg.S, cfg.D)


_NC_CACHE = {}


def kernel(**inputs):
    cfg = Cfg()
    if "nc" not in _NC_CACHE:
        _NC_CACHE["nc"] = build(cfg)
    nc = _NC_CACHE["nc"]
    maps = make_in_maps(cfg, {k: np.asarray(v) for k, v in inputs.items()})
    res = run_bass_kernel_spmd(nc, maps, core_ids=list(range(8)))
    return assemble(cfg, res.results)
```
